# Optimizing a Trainium2 kernel written in Bass

```python
import math
import jax, jax.numpy as jnp
from jax import lax
import numpy as np

D_MODEL = 1024
BATCH = 16
SEQ = 4096
DEPTH = 1
DEC_BATCH = 32
DEC_SEQ = 32
PAST_LEN = 1024

CHUNK = 64
D_CONV = 512
CONV_A_WIDTH = 3
N_HEADS_GDN = 4
HEAD_DIM = 128
D_GDN = N_HEADS_GDN * HEAD_DIM
D_QKV = 3 * D_GDN
CONV_QKV_WIDTH = 4
D_MIX = D_CONV + D_GDN
PROJ_SIZES = (D_CONV, D_CONV, D_CONV, D_CONV, D_QKV, D_GDN, N_HEADS_GDN, N_HEADS_GDN)
D_PROJ = 4 * D_CONV + D_QKV + D_GDN + 2 * N_HEADS_GDN
RMS_EPS = 1e-6
L2_EPS = 1e-6

kernel_name = "hybrid_shortconv_gated_deltanet_stream_step"


def rmsnorm(x, w):
    xf = x.astype(jnp.float32)
    var = jnp.mean(xf * xf, axis=-1, keepdims=True)
    return (xf * lax.rsqrt(var + RMS_EPS) * w.astype(jnp.float32)).astype(x.dtype)


def l2norm(x):
    xf = x.astype(jnp.float32)
    return xf * lax.rsqrt(jnp.sum(xf * xf, axis=-1, keepdims=True) + L2_EPS)


def causal_dwconv(u, buf, w):
    width = w.shape[0]
    t = u.shape[1]
    ucat = jnp.concatenate([buf.astype(u.dtype), u], axis=1)
    y = ucat[:, 0:t] * w[0]
    for j in range(1, width):
        y = y + ucat[:, j:j + t] * w[j]
    return y, ucat[:, t:]


def gated_delta_rule(q, k, v, g, beta, s0):
    f32 = jnp.float32
    bsz, t, h, dk = q.shape
    dv = v.shape[-1]
    c = math.gcd(t, CHUNK)
    n = t // c

    def blocks(a):
        a = a.astype(f32).reshape((bsz, n, c) + a.shape[2:])
        return jnp.moveaxis(a, 3, 1)

    qb = blocks(q) * (dk ** -0.5)
    kb = blocks(k)
    vb = blocks(v)
    gb = jnp.cumsum(blocks(g), axis=-1)
    bb = blocks(beta)

    incl = jnp.tril(jnp.ones((c, c), dtype=bool))
    strict = jnp.tril(jnp.ones((c, c), dtype=bool), -1)
    diff = gb[..., :, None] - gb[..., None, :]
    decay = jnp.where(incl, jnp.exp(jnp.where(incl, diff, 0.0)), 0.0)

    kbeta = kb * bb[..., None]
    a_mat = jnp.where(strict, jnp.einsum('bhnid,bhnjd->bhnij', kbeta, kb) * decay, 0.0)
    eye = jnp.broadcast_to(jnp.eye(c, dtype=f32), a_mat.shape)
    t_mat = lax.linalg.triangular_solve(a_mat, eye, left_side=True, lower=True,
                                        unit_diagonal=True)
    u = jnp.einsum('bhnij,bhnjd->bhnid', t_mat, vb * bb[..., None])
    w = jnp.einsum('bhnij,bhnjd->bhnid', t_mat, kbeta * jnp.exp(gb)[..., None])
    qg = qb * jnp.exp(gb)[..., None]
    attn = jnp.einsum('bhnid,bhnjd->bhnij', qb, kb) * decay
    g_last = gb[..., -1]
    kdec = kb * jnp.exp(g_last[..., None] - gb)[..., None]

    def step(s, xs):
        u_i, w_i, qg_i, attn_i, kdec_i, gl_i = xs
        v_new = u_i - jnp.einsum('bhcd,bhde->bhce', w_i, s)
        o_i = (jnp.einsum('bhcd,bhde->bhce', qg_i, s)
               + jnp.einsum('bhij,bhje->bhie', attn_i, v_new))
        s = s * jnp.exp(gl_i)[..., None, None] + jnp.einsum('bhcd,bhce->bhde', kdec_i, v_new)
        return s, o_i

    xs = (jnp.moveaxis(u, 2, 0), jnp.moveaxis(w, 2, 0), jnp.moveaxis(qg, 2, 0),
          jnp.moveaxis(attn, 2, 0), jnp.moveaxis(kdec, 2, 0), jnp.moveaxis(g_last, 2, 0))
    s_fin, o = lax.scan(step, s0.astype(f32), xs)
    o = jnp.moveaxis(o, 0, 2).reshape(bsz, h, t, dv).transpose(0, 2, 1, 3)
    return o, s_fin


def hybrid_layer(x, buf_a, buf_qkv, s0, ln_pre, w_in, conv_a_w, conv_qkv_w,
                 a_log, dt_bias, gdn_norm_w, w_out, ln_post):
    f32 = jnp.float32
    bsz, t, _ = x.shape
    h = rmsnorm(x, ln_pre)
    proj = jnp.einsum('btd,dp->btp', h, w_in)
    splits = np.cumsum(np.array(PROJ_SIZES))[:-1].tolist()
    b_a, c_a, x_a, z_a, qkv, z_g, a_dec, b_beta = jnp.split(proj, splits, axis=-1)

    conv_a, new_buf_a = causal_dwconv(c_a * x_a, buf_a, conv_a_w)
    y_a = b_a * conv_a * jax.nn.silu(z_a)

    qkv, new_buf_qkv = causal_dwconv(qkv, buf_qkv, conv_qkv_w)
    qkv = jax.nn.silu(qkv)
    q, k, v = jnp.split(qkv, 3, axis=-1)

    def heads(a):
        return a.reshape(bsz, t, N_HEADS_GDN, HEAD_DIM)

    q = l2norm(heads(q))
    k = l2norm(heads(k))
    v = heads(v)
    g = -jnp.exp(a_log.astype(f32)) * jax.nn.softplus(a_dec.astype(f32) + dt_bias.astype(f32))
    beta = jax.nn.sigmoid(b_beta.astype(f32))
    o, s_new = gated_delta_rule(q, k, v, g, beta, s0)
    o = rmsnorm(o, gdn_norm_w) * jax.nn.silu(heads(z_g).astype(f32))
    y_g = o.reshape(bsz, t, D_GDN).astype(x.dtype)

    mix = jnp.concatenate([y_a, y_g], axis=-1)
    out = jnp.einsum('btm,md->btd', mix, w_out)
    y = x + rmsnorm(out, ln_post)
    return y, new_buf_a, new_buf_qkv, s_new.astype(s0.dtype)


def setup_inputs(seed: int = 0) -> dict:
    key = jax.random.key(seed)
    ks = jax.random.split(key, 16)
    f32 = jnp.float32
    x_prompt = jax.random.normal(ks[0], (BATCH, SEQ, D_MODEL), f32)
    x_sample = jax.random.normal(ks[1], (DEC_BATCH, DEC_SEQ, D_MODEL), f32)
    cache_conv_a = jax.random.normal(ks[2], (DEPTH, DEC_BATCH, CONV_A_WIDTH - 1, D_CONV), f32)
    cache_conv_qkv = jax.random.normal(ks[3], (DEPTH, DEC_BATCH, CONV_QKV_WIDTH - 1, D_QKV), f32)
    state_gdn = 0.1 * jax.random.normal(ks[4], (DEPTH, DEC_BATCH, N_HEADS_GDN, HEAD_DIM, HEAD_DIM), f32)
    ln_pre = 1.0 + 0.01 * jax.random.normal(ks[5], (DEPTH, D_MODEL), f32)
    w_in = jax.random.normal(ks[6], (DEPTH, D_MODEL, D_PROJ), f32) * D_MODEL ** -0.5
    conv_a_w = jax.random.normal(ks[7], (DEPTH, CONV_A_WIDTH, D_CONV), f32) * CONV_A_WIDTH ** -0.5
    conv_qkv_w = jax.random.normal(ks[8], (DEPTH, CONV_QKV_WIDTH, D_QKV), f32) * CONV_QKV_WIDTH ** -0.5
    a_log = jnp.log(jax.random.uniform(ks[9], (DEPTH, N_HEADS_GDN), f32, 1.0, 16.0))
    dt_bias = 0.1 * jax.random.normal(ks[10], (DEPTH, N_HEADS_GDN), f32)
    gdn_norm_w = 1.0 + 0.01 * jax.random.normal(ks[11], (DEPTH, HEAD_DIM), f32)
    w_out = jax.random.normal(ks[12], (DEPTH, D_MIX, D_MODEL), f32) * D_MIX ** -0.5
    ln_post = 1.0 + 0.01 * jax.random.normal(ks[13], (DEPTH, D_MODEL), f32)
    return {"x_prompt": x_prompt, "x_sample": x_sample,
            "cache_conv_a": cache_conv_a, "cache_conv_qkv": cache_conv_qkv, "state_gdn": state_gdn,
            "ln_pre": ln_pre, "w_in": w_in, "conv_a_w": conv_a_w, "conv_qkv_w": conv_qkv_w,
            "a_log": a_log, "dt_bias": dt_bias, "gdn_norm_w": gdn_norm_w,
            "w_out": w_out, "ln_post": ln_post}


def reference(x_prompt, x_sample, cache_conv_a, cache_conv_qkv, state_gdn,
              ln_pre, w_in, conv_a_w, conv_qkv_w, a_log, dt_bias, gdn_norm_w, w_out, ln_post):
    y_p = x_prompt
    y_s = x_sample
    p_a, p_qkv, p_s = [], [], []
    s_a, s_qkv, s_s = [], [], []
    for layer in range(DEPTH):
        params = (ln_pre[layer], w_in[layer], conv_a_w[layer], conv_qkv_w[layer],
                  a_log[layer], dt_bias[layer], gdn_norm_w[layer], w_out[layer], ln_post[layer])
        bsz = y_p.shape[0]
        zero_a = jnp.zeros((bsz, CONV_A_WIDTH - 1, D_CONV), y_p.dtype)
        zero_qkv = jnp.zeros((bsz, CONV_QKV_WIDTH - 1, D_QKV), y_p.dtype)
        zero_s = jnp.zeros((bsz, N_HEADS_GDN, HEAD_DIM, HEAD_DIM), state_gdn.dtype)
        y_p, ba, bq, st = hybrid_layer(y_p, zero_a, zero_qkv, zero_s, *params)
        p_a.append(ba)
        p_qkv.append(bq)
        p_s.append(st)
        y_s, ba, bq, st = hybrid_layer(y_s, cache_conv_a[layer], cache_conv_qkv[layer],
                                       state_gdn[layer], *params)
        s_a.append(ba)
        s_qkv.append(bq)
        s_s.append(st)
    return (y_p, y_s, jnp.stack(p_a), jnp.stack(p_qkv), jnp.stack(p_s),
            jnp.stack(s_a), jnp.stack(s_qkv), jnp.stack(s_s))
```

```python
from contextlib import ExitStack

import numpy as np
import concourse.bass as bass
import concourse.mybir as mybir
from concourse.bass_utils import run_bass_kernel_spmd

F32 = mybir.dt.float32
BF16 = mybir.dt.bfloat16
AF = mybir.ActivationFunctionType
ALU = mybir.AluOpType

ENGS = ("pe", "act", "dve", "pool", "sp")

SB_BYTES = [0]
D = 1024
DP = 4104
NCORES = 8


class Op:
    __slots__ = ("eng", "fn", "reads", "writes", "deps", "signal", "count",
                 "is_dma", "dsem", "dval", "dprev", "idx")

    def __init__(self, eng, fn, reads, writes, is_dma):
        self.eng = eng
        self.fn = fn
        self.reads = reads
        self.writes = writes
        self.deps = []
        self.signal = False
        self.count = 0
        self.is_dma = is_dma
        self.dsem = None
        self.dval = 0
        self.dprev = 0
        self.idx = -1


class Sched:
    def __init__(self, nc, n_dma_sems=24):
        self.nc = nc
        self.ops = []
        self.last_w = {}
        self.readers = {}
        self.last_acc = {}
        self.n_dma_sems = n_dma_sems

    def op(self, eng, fn, reads=(), writes=(), dma=False):
        o = Op(eng, fn, tuple(reads), tuple(writes), dma)
        o.idx = len(self.ops)
        deps = {}
        banks = set(r for r in (o.reads + o.writes) if r.startswith("B:"))
        o.reads = tuple(r for r in o.reads if not r.startswith("B:"))
        o.writes = tuple(r for r in o.writes if not r.startswith("B:"))
        for bk in banks:
            la = self.last_acc.get(bk)
            if la is not None and not (la.eng == "pe" and eng == "pe"):
                deps[la.idx] = la
            self.last_acc[bk] = o
        for r in o.reads:
            w = self.last_w.get(r)
            if w is not None:
                deps[w.idx] = w
        for r in o.writes:
            w = self.last_w.get(r)
            if w is not None:
                deps[w.idx] = w
            for rd in self.readers.get(r, ()):
                deps[rd.idx] = rd
        for r in o.reads:
            self.readers.setdefault(r, []).append(o)
        for r in o.writes:
            self.last_w[r] = o
            self.readers[r] = []
        deps.pop(o.idx, None)
        o.deps = list(deps.values())
        for d in o.deps:
            d.signal = True
        self.ops.append(o)
        return o

    def emit(self, final_wait_ops=()):
        nc = self.nc
        cnt = {e: 0 for e in ENGS}
        dma_use = [0] * self.n_dma_sems
        dma_rr = 0
        for o in self.ops:
            if o.is_dma:
                s = dma_rr % self.n_dma_sems
                dma_rr += 1
                o.dsem = s
                o.dprev = dma_use[s] * 16
                dma_use[s] += 1
                o.dval = dma_use[s] * 16
            elif o.signal:
                cnt[o.eng] += 1
                o.count = cnt[o.eng]
        with ExitStack() as es:
            esem = {e: es.enter_context(nc.semaphore("s_" + e)) for e in ENGS if e != "sp"}
            dsems = [es.enter_context(nc.semaphore("d%d" % i)) for i in range(self.n_dma_sems)]
            block = es.enter_context(nc.Block())
            per_eng = {e: [o for o in self.ops if o.eng == e] for e in ENGS}

            def run(ename, eng):
                waited = {}

                def need(key, sem, val):
                    if waited.get(key, 0) >= val:
                        return
                    waited[key] = val
                    eng.wait_ge(sem, val)

                for o in per_eng[ename]:
                    for d in o.deps:
                        if d.is_dma:
                            need(("d", d.dsem), dsems[d.dsem], d.dval)
                        else:
                            need(d.eng, esem[d.eng], d.count)
                    if o.is_dma:
                        if o.dprev:
                            need(("d", o.dsem), dsems[o.dsem], o.dprev)
                        ins = o.fn(eng)
                        ins.then_inc(dsems[o.dsem], 16)
                    else:
                        ins = o.fn(eng)
                        if o.signal:
                            ins.then_inc(esem[ename], 1)
                if ename == "sp":
                    for o in final_wait_ops:
                        need(("d", o.dsem), dsems[o.dsem], o.dval)

            @block.tensor
            def _(e):
                run("pe", e)

            @block.scalar
            def _(e):
                run("act", e)

            @block.vector
            def _(e):
                run("dve", e)

            @block.gpsimd
            def _(e):
                run("pool", e)

            @block.sync
            def _(e):
                run("sp", e)


C_ID, C_ONE, C_PMI, C_PMS, C_PSL, C_SMI, C_SMS, C_SSL, C_SEG = (
    0, 128, 256, 384, 512, 640, 768, 896, 1024)
C_TOT = 1028


def make_consts():
    c = np.zeros((128, C_TOT), np.float32)
    idx = np.arange(128)
    c[:, C_ID:C_ID + 128] = np.eye(128)
    c[:, C_ONE:C_ONE + 128] = 1.0
    for base, segsz in ((C_PMI, 128), (C_SMI, 32)):
        same = (idx[:, None] // segsz) == (idx[None, :] // segsz)
        c[:, base:base + 128] = ((idx[None, :] >= idx[:, None]) & same)
        c[:, base + 128:base + 256] = ((idx[None, :] > idx[:, None]) & same)
        c[:, base + 256:base + 384] = ((idx[:, None] > idx[None, :]) & same)
    for s in range(4):
        c[:, C_SEG + s] = (idx // 32 == s)
    return c


def build_program(n_pseq=2, seq_len=4096, n_sseq=4, s_len=32):
    NT = 256
    nblk = seq_len // NT
    assert seq_len % NT == 0 and n_sseq * s_len == 128
    nc = bass.Bass("TRN2", target_bir_lowering=False)

    def din(name, shape):
        return nc.dram_tensor(name, list(shape), F32, kind="ExternalInput").ap()

    def dout(name, shape):
        return nc.dram_tensor(name, list(shape), F32, kind="ExternalOutput").ap()

    xp = din("xp", [n_pseq, seq_len, D])
    xs = din("xs", [n_sseq * s_len, D])
    cca = din("cca", [n_sseq, 2, 512])
    ccq = din("ccq", [n_sseq, 3, 1536])
    sg = din("sg", [n_sseq, 4, 128, 128])
    ln_pre = din("ln_pre", [D])
    w_in = din("w_in", [D, DP])
    caw = din("caw", [3, 512])
    cqw = din("cqw", [4, 1536])
    a_log = din("a_log", [4])
    dt_bias = din("dt_bias", [4])
    gnw_d = din("gnw", [128])
    w_out = din("w_out", [D, D])
    ln_post = din("ln_post", [D])
    cst_d = din("cst", [128, C_TOT])

    yp = dout("yp", [n_pseq, seq_len, D])
    ys = dout("ys", [n_sseq * s_len, D])
    o_cap = dout("o_cap", [n_pseq, 2, 512])
    o_cqp = dout("o_cqp", [n_pseq, 3, 1536])
    o_gp = dout("o_gp", [n_pseq, 4, 128, 128])
    o_cas = dout("o_cas", [n_sseq, 2, 512])
    o_cqs = dout("o_cqs", [n_sseq, 3, 1536])
    o_gs = dout("o_gs", [n_sseq, 4, 128, 128])

    es = ExitStack()
    with es:
        def sb(name, shape, dt=F32):
            n = 1
            for d_ in shape[1:]:
                n *= d_
            SB_BYTES[0] += n * (2 if dt == BF16 else 4)
            return es.enter_context(nc.sbuf_tensor("sb_" + name, list(shape), dt))

        def ps(name, shape, dt=F32):
            return es.enter_context(nc.psum_tensor("ps_" + name, list(shape), dt))

        S = Sched(nc)
        final_ops = []

        w_in_bf = sb("w_in_bf", [128, 8, DP], BF16)
        w_out_bf = sb("w_out_bf", [128, 8, D], BF16)
        cst = sb("cst", [128, C_TOT])
        ident_bf = sb("ident_bf", [128, 128], BF16)
        ones_bf = sb("ones_bf", [128, 2], BF16)
        lnpre_t = sb("lnpre_t", [128, 8])
        caw_t = sb("caw_t", [128, 4, 3])
        cqw_t = sb("cqw_t", [128, 12, 4])
        gnw_t = sb("gnw_t", [128, 1])
        lnpost_bc = sb("lnpost_bc", [128, D])
        dtb_bc = sb("dtb_bc", [128, 4])
        negA_bc = sb("negA_bc", [128, 4])
        xt = [sb("xt%d" % i, [128, 2, D]) for i in range(2)]
        junk = sb("junk", [128, 512], BF16)
        hb = sb("hb", [128, D], BF16)
        hT = sb("hT", [128, 8, NT], BF16)
        ss1 = sb("ss1", [128, 2])
        lt1 = sb("lt1", [128, 2])
        rstd1 = sb("rstd1", [128, 2])
        ca_sb = sb("ca_sb", [128, NT])
        ua = sb("ua", [128, 4 * (2 + NT)])
        acc_a = sb("acc_a", [128, NT])
        sza = sb("sza", [128, NT])
        pre = sb("pre", [128, 12 * (3 + NT)])
        accq = [sb("accq%d" % i, [128, NT]) for i in range(3)]
        qkvT = [sb("qkvT%d" % i, [128, 12, NT], BF16) for i in range(2)]
        szg = [sb("szg%d" % i, [128, 4, NT], BF16) for i in range(2)]
        mixT = [sb("mixT%d" % i, [128, 8, NT], BF16) for i in range(2)]
        ab_sb = [sb("ab_sb%d" % i, [128, 2, 8]) for i in range(2)]
        TINY = []
        for i in range(2):
            TINY.append(dict(
                abx=sb("abx%d" % i, [128, 4]), e1=sb("e1%d" % i, [128, 4]), sp_t=sb("sp_t%d" % i, [128, 4]),
                g_t=sb("g_t%d" % i, [128, 4]), gseg=sb("gseg%d" % i, [128, 16]), e2=sb("e2%d" % i, [128, 4]),
                nbeta=sb("nbeta%d" % i, [128, 4]), beta=sb("beta%d" % i, [128, 4]), egb=sb("egb%d" % i, [128, 4]),
                negb=sb("negb%d" % i, [128, 4]), ekd=sb("ekd%d" % i, [128, 4]), egl=sb("egl%d" % i, [128, 16])))
        gUs = sb("gUs", [128, 4, 128])
        EG = sb("EG", [128, 4, 128])
        ssk = sb("ssk", [128, 4])
        lk = sb("lk", [128, 4])
        rk = sb("rk", [128, 4])
        sso = sb("sso", [128, 4])
        fa = sb("fa", [128, 4])
        fb = sb("fb", [128, 4])
        fl = sb("fl", [128, 4])
        ff = sb("ff", [128, 4])
        H = []
        for h in range(4):
            d = dict(
                khat=sb("khat%d" % h, [128, 128], BF16),
                vtm=sb("vtm%d" % h, [128, 128], BF16),
                khatT=sb("khatT%d" % h, [128, 128], BF16),
                qsq=sb("qsq%d" % h, [128, 128], BF16),
                nm=[sb("nm%d_%d" % (h, i), [128, 384], BF16) for i in range(2)],
                attnT=sb("attnT%d" % h, [128, 128], BF16),
                r=sb("r%d" % h, [128, 128], BF16),
                kdec=sb("kdec%d" % h, [128, 128], BF16),
                o=sb("o%d" % h, [128, 128]),
            )
            H.append(d)
        khm = sb("khm", [128, 4, 4, 128], BF16)
        qm = sb("qm", [128, 4, 4, 128], BF16)
        junkh = [sb("junkh%d" % i, [128, 128], BF16) for i in range(4)]
        S32 = sb("S32", [128, 16, 128])
        Sbf = sb("Sbf", [128, 16, 128], BF16)
        ss2 = sb("ss2", [128, 2])
        ss2s = sb("ss2s", [128, 1])
        lt2 = sb("lt2", [128, 1])
        rstd2 = sb("rstd2", [128, 1])
        ytmp = [sb("ytmp%d" % i, [128, D]) for i in range(2)]

        pT = ps("pT", [128, 8, 128], BF16)
        pP = [ps("pP%d" % i, [128, 512]) for i in range(2)]
        pO = pP
        pD = ps("pD", [128, 512])
        pHb = [ps("pH%d" % i, [128, 512]) for i in range(4)]
        BK_T, BK_D = "B:T", "B:D"
        BK_P = ["B:P0", "B:P1"]
        BK_H = ["B:H%d" % i for i in range(4)]
        pTiny = pD[:, 256:384]
        R_TINY = BK_D

        def cs(off, n=128):
            return cst[:, off:off + n]

        def dma(out, in_, r=(), w=(), nonc=False, final=False):
            if nonc:
                o = S.op("sp", lambda e: e.dma_start(out=out, in_=in_, allow_slow_non_contiguous=True), r, w, dma=True)
            else:
                o = S.op("sp", lambda e: e.dma_start(out=out, in_=in_), r, w, dma=True)
            if final:
                final_ops.append(o)
            return o

        def ts(eng, out, in0, s1, op0, s2=None, op1=None, r=(), w=()):
            if op1 is None and eng == "pool" and op0 == ALU.mult:
                s2, op1 = 0.0, ALU.add
            if op1 is None:
                return S.op(eng, lambda e: e.tensor_scalar(out=out, in0=in0, scalar1=s1, scalar2=None, op0=op0), r, w)
            return S.op(eng, lambda e: e.tensor_scalar(out=out, in0=in0, scalar1=s1, scalar2=s2, op0=op0, op1=op1), r, w)

        def tt(eng, out, in0, in1, op, r=(), w=()):
            return S.op(eng, lambda e: e.tensor_tensor(out=out, in0=in0, in1=in1, op=op), r, w)

        def stt(out, in0, scalar, in1, op0, op1, r=(), w=()):
            return S.op("dve", lambda e: e.scalar_tensor_tensor(out=out, in0=in0, scalar=scalar, in1=in1, op0=op0, op1=op1), r, w)

        def act(out, in_, func, scale=None, bias=None, accum=None, r=(), w=()):
            kw = {}
            if scale is not None:
                kw["scale"] = scale
            if bias is not None:
                kw["bias"] = bias
            if accum is not None:
                kw["accum_out"] = accum
            return S.op("act", lambda e: e.activation(out=out, in_=in_, func=func, **kw), r, w)

        def cp(eng, out, in_, r=(), w=()):
            if eng == "act":
                return S.op("act", lambda e: e.copy(out=out, in_=in_), r, w)
            return S.op(eng, lambda e: e.tensor_copy(out=out, in_=in_), r, w)

        def mms(lst, r=(), w=()):
            def f(e):
                ins = None
                for (out, lhsT, rhs, st, sp) in lst:
                    ins = e.matmul(out, lhsT=lhsT, rhs=rhs, start=st, stop=sp)
                return ins
            return S.op("pe", f, r, w)

        def trs(lst, r=(), w=()):
            def f(e):
                ins = None
                for (out, in_) in lst:
                    ins = e.transpose(out=out, in_=in_, identity=ident_bf[:])
                return ins
            return S.op("pe", f, tuple(r) + ("ident_bf",), w)

        def memset(eng, ap, val, w=()):
            return S.op(eng, lambda e: e.memset(ap, val), (), w)

        dma(cst[:], cst_d, w=["cst"])
        dma(lnpre_t[:], ln_pre.rearrange("(k p) -> p k", p=128), w=["lnpre"], nonc=True)
        for j in range(3):
            dma(caw_t[:, :, j], caw[j].rearrange("(c p) -> p c", p=128), w=["caw"], nonc=True)
        for j in range(4):
            dma(cqw_t[:, :, j], cqw[j].rearrange("(c p) -> p c", p=128), w=["cqw"], nonc=True)
        dma(gnw_t[:], gnw_d.rearrange("(p o) -> p o", o=1), w=["gnw"], nonc=True)
        dma(lnpost_bc[:], ln_post.partition_broadcast(128), w=["lnpost"])
        dma(dtb_bc[:], dt_bias.partition_broadcast(128), w=["dtb"])
        dma(negA_bc[:], a_log.partition_broadcast(128), w=["negA"])
        act(negA_bc[:], negA_bc[:], AF.Exp, r=["negA"], w=["negA"])
        ts("dve", negA_bc[:], negA_bc[:], -1.0, ALU.mult, r=["negA"], w=["negA"])
        cp("dve", ident_bf[:], cs(C_ID), r=["cst"], w=["ident_bf"])
        memset("dve", ones_bf[:], 1.0, w=["ones_bf"])
        memset("pool", khm[:], 0.0, w=["khm%d" % h for h in range(4)])
        memset("pool", qm[:], 0.0, w=["qm%d" % h for h in range(4)])
        PIECE = 1024
        f32v = lambda t, pat: t[:].rearrange(pat).bitcast(F32)
        stg = [(ytmp[0][:], ["ytmp0"]), (ytmp[1][:], ["ytmp1"]),
               (f32v(mixT[1], "p a b -> p (a b)"), ["mixT1_%d" % i for i in range(8)]),
               (S32[:, 0:8, :].rearrange("p a b -> p (a b)"), ["S32_%d" % i for i in range(8)]),
               (S32[:, 8:16, :].rearrange("p a b -> p (a b)"), ["S32_%d" % i for i in range(8, 16)]),
               (xt[1][:, 0, :], ["xt1"]),
               (f32v(qkvT[1], "p a b -> p (a b)")[:, 0:PIECE], ["qkvT1_%d" % i for i in range(12)])]
        abst = junk[:, 0:128].bitcast(F32).rearrange("p (k c) -> p k c", k=8)
        dma(abst, w_in[:, 4096:4104].rearrange("(k p) c -> p k c", p=128), w=["junk"], nonc=True)
        for k in range(8):
            ts("dve", w_in_bf[:, k, 4096:4104], abst[:, k, :], lnpre_t[:, k:k + 1], ALU.mult, r=["junk", "lnpre"],
               w=["w_in_bf_ab"])
        si = 0
        for pi in (2, 3, 0, 1):
            for k in range(8):
                c0 = pi * PIECE
                sv, rn = stg[si % len(stg)]
                stv = sv[:, 0:PIECE]
                dma(stv, w_in[k * 128:(k + 1) * 128, c0:c0 + PIECE], w=rn)
                eng = ("dve", "pool")[si % 2]
                ts(eng, w_in_bf[:, k, c0:c0 + PIECE], stv, lnpre_t[:, k:k + 1], ALU.mult,
                   r=rn + ["lnpre"], w=["w_in_bf_p%d" % pi])
                si += 1
        for k in range(8):
            sv, rn = stg[si % len(stg)]
            stv = sv[:, 0:D]
            dma(stv, w_out[k * 128:(k + 1) * 128, :], w=rn)
            eng = ("dve", "pool")[si % 2]
            cp(eng, w_out_bf[:, k, :], stv, r=rn, w=["w_out_bf"])
            si += 1

        def wres(c0, c1):
            if c0 >= 4096:
                return ["w_in_bf_ab"]
            return ["w_in_bf_p%d" % p for p in range(c0 // PIECE, (c1 - 1) // PIECE + 1)]

        UA_ALL = ["ua%d" % i for i in range(4)]
        PRE_ALL = ["pre%d" % i for i in range(12)]

        def MIX_ALL(pb):
            return ["mixT%d_%d" % (pb, i) for i in range(8)]

        def load_x(bi, src_ap, ntile):
            buf = bi % 2
            dma(xt[buf][:, 0:ntile, :], src_ap.rearrange("(j p) d -> p j d", p=128), w=["xt%d" % buf])

        def gen_a(bi, ntile):
            buf = bi % 2
            xr = "xt%d" % buf
            for j in range(ntile):
                act(hb[:], xt[buf][:, j, :], AF.Square, accum=ss1[:, j:j + 1], r=[xr], w=["hb", "ss1"])
            act(lt1[:, 0:ntile], ss1[:, 0:ntile], AF.Ln, scale=1.0 / D, bias=1e-6, r=["ss1"], w=["lt1"])
            act(rstd1[:, 0:ntile], lt1[:, 0:ntile], AF.Exp, scale=-0.5, r=["lt1"], w=["rstd1"])
            yield
            for j in range(ntile):
                ts("dve", hb[:], xt[buf][:, j, :], rstd1[:, j:j + 1], ALU.mult, r=[xr, "rstd1"], w=["hb"])
                trs([(pT[:, k, :], hb[:, k * 128:(k + 1) * 128]) for k in range(8)], r=["hb"], w=[BK_T])
                cp("act", hT[:, :, j * 128:(j + 1) * 128], pT[:], r=[BK_T], w=["hT"])
                yield

        ppi = [0]

        def proj_chunk(c, ntok):
            i = ppi[0] % 2
            ppi[0] += 1
            mms([(pP[i][:, 0:ntok], w_in_bf[:, k, c * 128:(c + 1) * 128], hT[:, k, 0:ntok], k == 0, k == 7)
                 for k in range(8)], r=wres(c * 128, c * 128 + 128) + ["hT"], w=[BK_P[i]])
            return pP[i], BK_P[i]

        aqi = [0]

        def gen_b(bi, ntok, nseg):
            pb = bi % 2
            ntile = ntok // 128
            L = ntok // nseg
            ua_v = ua[:, 0:4 * nseg * (2 + L)].rearrange("p (c s l) -> p c s l", c=4, s=nseg)
            pre_v = pre[:, 0:12 * nseg * (3 + L)].rearrange("p (c s l) -> p c s l", c=12, s=nseg)
            qk, sz, mx = qkvT[pb], szg[pb], mixT[pb]

            def v3(ap2):
                return ap2.rearrange("p (s l) -> p s l", s=nseg)

            jobs = []

            def mm_stage(c):
                def f(bk):
                    mms([(pP[bk][:, 0:ntok], w_in_bf[:, k, c * 128:(c + 1) * 128], hT[:, k, 0:ntok], k == 0, k == 7)
                         for k in range(8)], r=wres(c * 128, c * 128 + 128) + ["hT"], w=[BK_P[bk]])
                return f

            for j in range(ntile):
                def s0(bk, j=j):
                    mms([(pP[bk][:, 0:8], hT[:, k, j * 128:(j + 1) * 128], w_in_bf[:, k, 4096:4104], k == 0, k == 7)
                         for k in range(8)], r=["hT"] + wres(4096, 4104), w=[BK_P[bk]])

                def s1(bk, j=j):
                    cp("dve", ab_sb[pb][:, j, :], pP[bk][:, 0:8], r=[BK_P[bk]], w=["ab%d_%d" % (pb, j)])
                jobs.append([s0, s1])
            for c in list(range(4, 8)) + list(range(8, 12)) + list(range(0, 4)):
                ai = aqi[0] % 3
                aqi[0] += 1
                aq, ar = accq[ai], "accq%d" % ai

                def s1(bk, c=c):
                    cp("act", pre_v[:, c, :, 3:3 + L], v3(pP[bk][:, 0:ntok]), r=[BK_P[bk]], w=["pre%d" % c])

                def s2(bk, c=c, aq=aq, ar=ar):
                    ts("pool", v3(aq[:, 0:ntok]), pre_v[:, c, :, 0:L], cqw_t[:, c, 0:1], ALU.mult,
                       r=["pre%d" % c, "cqw"], w=[ar])

                def s3(bk, c=c, aq=aq, ar=ar):
                    for j in range(1, 4):
                        stt(v3(aq[:, 0:ntok]), pre_v[:, c, :, j:j + L], cqw_t[:, c, j:j + 1], v3(aq[:, 0:ntok]),
                            ALU.mult, ALU.add, r=["pre%d" % c, "cqw", ar], w=[ar])

                def s4(bk, c=c, aq=aq, ar=ar):
                    act(qk[:, c, 0:ntok], aq[:, 0:ntok], AF.Silu, r=[ar], w=["qkvT%d_%d" % (pb, c)])
                jobs.append([mm_stage(16 + c), s1, s2, s3, s4])
            for c in range(4):
                def s1(bk, c=c):
                    act(sz[:, c, 0:ntok], pP[bk][:, 0:ntok], AF.Silu, r=[BK_P[bk]], w=["szg%d_%d" % (pb, c)])
                jobs.append([mm_stage(28 + c), s1])
            for cg in range(4):
                def c1(bk):
                    cp("act", ca_sb[:, 0:ntok], pP[bk][:, 0:ntok], r=[BK_P[bk]], w=["ca_sb"])
                jobs.append([mm_stage(4 + cg), c1])

                def x1(bk, cg=cg):
                    tt("dve", ua_v[:, cg, :, 2:2 + L], v3(pP[bk][:, 0:ntok]), v3(ca_sb[:, 0:ntok]), ALU.mult,
                       r=[BK_P[bk], "ca_sb"], w=["ua%d" % cg])

                def x2(bk, cg=cg):
                    ts("pool", v3(acc_a[:, 0:ntok]), ua_v[:, cg, :, 0:L], caw_t[:, cg, 0:1], ALU.mult,
                       r=["ua%d" % cg, "caw"], w=["acc_a"])

                def x3(bk, cg=cg):
                    for j in range(1, 3):
                        stt(v3(acc_a[:, 0:ntok]), ua_v[:, cg, :, j:j + L], caw_t[:, cg, j:j + 1],
                            v3(acc_a[:, 0:ntok]), ALU.mult, ALU.add, r=["ua%d" % cg, "caw", "acc_a"], w=["acc_a"])
                jobs.append([mm_stage(8 + cg), x1, x2, x3])

                def z1(bk):
                    act(sza[:, 0:ntok], pP[bk][:, 0:ntok], AF.Silu, r=[BK_P[bk]], w=["sza"])
                jobs.append([mm_stage(12 + cg), z1])

                def b1(bk):
                    tt("dve", acc_a[:, 0:ntok], pP[bk][:, 0:ntok], acc_a[:, 0:ntok], ALU.mult, r=[BK_P[bk], "acc_a"],
                       w=["acc_a"])

                def b2(bk, cg=cg):
                    tt("pool", mx[:, cg, 0:ntok], acc_a[:, 0:ntok], sza[:, 0:ntok], ALU.mult,
                       r=["acc_a", "sza"], w=["mixT%d_%d" % (pb, cg)])
                jobs.append([mm_stage(cg), b1, b2])
            nj = len(jobs)
            base = ppi[0]
            ppi[0] += nj
            for u in range(nj + 4):
                for st in (4, 3, 2, 1, 0):
                    ji = u - st
                    if 0 <= ji < nj and st < len(jobs[ji]) and jobs[ji][st] is not None:
                        jobs[ji][st]((base + ji) % 2)
                yield

        def conv_tail(blk, nseg, L):
            ua_v = ua[:, 0:4 * nseg * (2 + L)].rearrange("p (c s l) -> p c s l", c=4, s=nseg)
            pre_v = pre[:, 0:12 * nseg * (3 + L)].rearrange("p (c s l) -> p c s l", c=12, s=nseg)
            if blk["kind"] == "p":
                if blk["last"]:
                    sq = blk["sq"]
                    for t in range(2):
                        dma(o_cap[sq, t].rearrange("(c p) -> p c", p=128), ua_v[:, :, 0, L + t], r=UA_ALL, nonc=True, final=True)
                    for t in range(3):
                        dma(o_cqp[sq, t].rearrange("(c p) -> p c", p=128), pre_v[:, :, 0, L + t], r=PRE_ALL, nonc=True, final=True)
                else:
                    cp("pool", ua_v[:, :, :, 0:2], ua_v[:, :, :, L:L + 2], r=UA_ALL, w=UA_ALL)
                    cp("pool", pre_v[:, :, :, 0:3], pre_v[:, :, :, L:L + 3], r=PRE_ALL, w=PRE_ALL)
            else:
                for s in range(4):
                    for t in range(2):
                        dma(o_cas[s, t].rearrange("(c p) -> p c", p=128), ua_v[:, :, s, L + t], r=UA_ALL, nonc=True, final=True)
                    for t in range(3):
                        dma(o_cqs[s, t].rearrange("(c p) -> p c", p=128), pre_v[:, :, s, L + t], r=PRE_ALL, nonc=True, final=True)

        def tile_units(bi, j, nseg, sbase, nlev):
            pb = bi % 2
            tp_ = j % 2
            qk, sz, mx = qkvT[pb], szg[pb], mixT[pb]
            tsl = slice(j * 128, (j + 1) * 128)
            HS = range(4)
            if nseg == 1:
                MI, MS, SL = cs(C_PMI), cs(C_PMS), cs(C_PSL)
            else:
                MI, MS, SL = cs(C_SMI), cs(C_SMS), cs(C_SSL)
            abr = "ab%d_%d" % (pb, j)
            T = TINY[tp_]

            def n(x):
                return "%s_%d" % (x, tp_)
            g_t, beta, nbeta, egb, negb, ekd, egl = (T["g_t"], T["beta"], T["nbeta"], T["egb"], T["negb"],
                                                    T["ekd"], T["egl"])

            def prepA():
                tt("dve", T["abx"][:], ab_sb[pb][:, j, 0:4], dtb_bc[:], ALU.add, r=[abr, "dtb"], w=[n("abx")])
                act(T["e2"][:], ab_sb[pb][:, j, 4:8], AF.Exp, scale=-1.0, r=[abr], w=[n("e2")])
                act(T["e1"][:], T["abx"][:], AF.Exp, r=[n("abx")], w=[n("e1")])
                act(T["sp_t"][:], T["e1"][:], AF.Ln, bias=1.0, r=[n("e1")], w=[n("sp_t")])
                tt("dve", g_t[:], T["sp_t"][:], negA_bc[:], ALU.mult, r=[n("sp_t"), "negA"], w=[n("g_t")])
                ts("dve", nbeta[:], T["e2"][:], -1.0, ALU.mult, -1.0, ALU.add, r=[n("e2")], w=[n("nbeta")])
                S.op("dve", lambda e: e.reciprocal(out=nbeta[:], in_=nbeta[:]), [n("nbeta")], [n("nbeta")])
                ts("dve", beta[:], nbeta[:], -1.0, ALU.mult, r=[n("nbeta")], w=[n("beta")])
                if nseg == 1:
                    gsv = g_t[:]
                else:
                    for s in range(nseg):
                        ts("dve", T["gseg"][:, s * 4:(s + 1) * 4], g_t[:], cst[:, C_SEG + s:C_SEG + s + 1], ALU.mult,
                           r=[n("g_t"), "cst"], w=[n("gseg")])
                    gsv = T["gseg"][:, 0:4 * nseg]
                po = 32 * tp_
                mms([(pTiny[:, po + 0:po + 4], MI, g_t[:], True, True),
                     (pTiny[:, po + 4:po + 8], SL, g_t[:], True, True),
                     (pTiny[:, po + 8:po + 8 + 4 * nseg], cs(C_ONE), gsv, True, True)],
                    r=["cst", n("g_t"), n("gseg")], w=[R_TINY])
                act(egb[:], pTiny[:, po + 0:po + 4], AF.Exp, r=[R_TINY], w=[n("egb")])
                act(ekd[:], pTiny[:, po + 4:po + 8], AF.Exp, r=[R_TINY], w=[n("ekd")])
                act(egl[:, 0:4 * nseg], pTiny[:, po + 8:po + 8 + 4 * nseg], AF.Exp, r=[R_TINY], w=[n("egl")])
                ts("dve", negb[:], egb[:], -1.0, ALU.mult, r=[n("egb")], w=[n("negb")])

            def prepB(hp):
                def f():
                    hh = (2 * hp, 2 * hp + 1)
                    for h in hh:
                        ts("pool", gUs[:, h, :], MI, g_t[:, h:h + 1], ALU.mult, r=["cst", n("g_t")], w=["gUs%d" % h])
                    mms([(pD[:, 0:256], SL, gUs[:, 2 * hp:2 * hp + 2, :].rearrange("p h i -> p (h i)"), True, True)],
                        r=["cst"] + ["gUs%d" % h for h in hh], w=[BK_D])
                    act(EG[:, 2 * hp:2 * hp + 2, :].rearrange("p h i -> p (h i)"), pD[:, 0:256], AF.Exp, r=[BK_D],
                        w=["EG%d" % h for h in hh])
                    for h in hh:
                        tt("pool", gUs[:, h, :], EG[:, h, :], MS, ALU.mult, r=["EG%d" % h, "cst"], w=["gUs%d" % h])
                        tt("pool", EG[:, h, :], EG[:, h, :], MI, ALU.mult, r=["EG%d" % h, "cst"], w=["EG%d" % h])
                return f
            GTs, GTi = gUs, EG

            def slots(h):
                pb_ = pHb[h]
                return dict(KV=pb_[:, 0:128].bitcast(BF16), TR=pb_[:, 448:512].bitcast(BF16), KQ=pb_[:, 0:256],
                            NM=pb_[:, 0:384], B=pb_[:, 0:128], C=pb_[:, 128:256], A=pb_[:, 256:384])

            PS_ = [slots(h) for h in HS]
            qTs = [qk[:, h, tsl] for h in HS]
            kTs = [qk[:, 4 + h, tsl] for h in HS]
            vTs = [qk[:, 8 + h, tsl] for h in HS]
            qr = ["qkvT%d_%d" % (pb, h) for h in HS]
            kr = ["qkvT%d_%d" % (pb, 4 + h) for h in HS]
            vr = ["qkvT%d_%d" % (pb, 8 + h) for h in HS]
            main = []

            def P1():
                for h in HS:
                    trs([(PS_[h]["KV"][:, 0:128], kTs[h]), (PS_[h]["KV"][:, 128:256], vTs[h])],
                        r=[kr[h], vr[h]], w=[BK_H[h]])
                    act(junkh[h][:], PS_[h]["KV"][:, 0:128], AF.Square, accum=ssk[:, h:h + 1], r=[BK_H[h]],
                        w=["junkh%d" % h, "ssk%d" % h])
                    tt("pool", H[h]["qsq"][:], qTs[h], qTs[h], ALU.mult, r=[qr[h]], w=["qsq%d" % h])
                act(lk[:], ssk[:], AF.Ln, bias=1e-6, r=["ssk%d" % h for h in HS], w=["lk"])
                act(rk[:], lk[:], AF.Exp, scale=-0.5, r=["lk"], w=["rk"])
            main.append(P1)

            def P2():
                for h in HS:
                    ts("dve", H[h]["khat"][:], PS_[h]["KV"][:, 0:128], rk[:, h:h + 1], ALU.mult, r=[BK_H[h], "rk"],
                       w=["khat%d" % h])
                    cp("act", H[h]["vtm"][:], PS_[h]["KV"][:, 128:256], r=[BK_H[h]], w=["vtm%d" % h])
                po = 16 + 32 * tp_
                mms([(pTiny[:, po + 2 * h:po + 2 * h + 2], H[h]["qsq"][:], ones_bf[:], True, True) for h in HS],
                    r=["qsq%d" % h for h in HS] + ["ones_bf"], w=[BK_D])
                ts("dve", fa[:], pTiny[:, po:po + 8:2], 1.28e-4, ALU.mult, 1.28e-10, ALU.add, r=[BK_D], w=[n("fa")])
            main.append(P2)

            def P3():
                for h in HS:
                    trs([(PS_[h]["TR"][:, 0:128], H[h]["khat"][:])], r=["khat%d" % h], w=[BK_H[h]])
                    cp("act" if h % 2 == 0 else "dve", H[h]["khatT"][:], PS_[h]["TR"][:, 0:128], r=[BK_H[h]],
                       w=["khatT%d" % h])
                    ts("pool", H[h]["kdec"][:], H[h]["khat"][:], ekd[:, h:h + 1], ALU.mult,
                       r=["khat%d" % h, n("ekd")], w=["kdec%d" % h])
            main.append(P3)

            def P4():
                for h in HS:
                    mms([(PS_[h]["KQ"][:, 0:128], H[h]["khatT"][:], H[h]["khatT"][:], True, True),
                         (PS_[h]["KQ"][:, 128:256], H[h]["khatT"][:], qTs[h], True, True)],
                        r=["khatT%d" % h, qr[h]], w=[BK_H[h]])
                    nm0 = H[h]["nm"][0]
                    stt(nm0[:, 128:256], PS_[h]["KQ"][:, 0:128], nbeta[:, h:h + 1], GTs[:, h, :], ALU.mult, ALU.mult,
                        r=[BK_H[h], n("nbeta"), "gUs%d" % h], w=["nm%d_0q" % h])
                    tt("dve", H[h]["attnT"][:], PS_[h]["KQ"][:, 128:256], GTi[:, h, :], ALU.mult,
                       r=[BK_H[h], "EG%d" % h], w=["attnT%d" % h])
            main.append(P4)

            def P5():
                for h in HS:
                    nm0 = H[h]["nm"][0]
                    trs([(PS_[h]["TR"][:, 0:128], nm0[:, 128:256])], r=["nm%d_0q" % h], w=[BK_H[h]])
                    cp("act" if h % 2 == 1 else "dve", nm0[:, 0:128], PS_[h]["TR"][:, 0:128], r=[BK_H[h]],
                       w=["nm%d_0p" % h])
                    tt("pool", nm0[:, 256:384], nm0[:, 128:256], ident_bf[:], ALU.add,
                       r=["nm%d_0q" % h, "ident_bf"], w=["nm%d_0t" % h])
            main.append(P5)

            def NL(lv):
                def f():
                    cur = (lv - 1) % 2
                    fin = lv == nlev + 1
                    for h in HS:
                        a_, b_ = H[h]["nm"][cur], H[h]["nm"][1 - cur]
                        Pp, Qp, Tp = a_[:, 0:128], a_[:, 128:256], a_[:, 256:384]
                        NM = PS_[h]["NM"]
                        lst = []
                        if not fin:
                            lst.append((NM[:, 0:128], Qp, Pp, True, True))
                            lst.append((NM[:, 128:256], Pp, Qp, True, True))
                        lst.append((NM[:, 256:384], ident_bf[:], Tp, True, lv == 1))
                        if lv > 1:
                            lst.append((NM[:, 256:384], Pp, Tp, False, True))
                        rr = ["nm%d_%d%s" % (h, cur, x) for x in "pqt"] + ["ident_bf"]
                        mms(lst, r=rr, w=[BK_H[h]])
                        eng = "act" if h % 2 == 0 else "dve"
                        if fin:
                            cp(eng, b_[:, 256:384], NM[:, 256:384], r=[BK_H[h]], w=["nm%d_%dt" % (h, 1 - cur)])
                        else:
                            cp(eng, b_[:, :], NM[:, :], r=[BK_H[h]], w=["nm%d_%d%s" % (h, 1 - cur, x) for x in "pqt"])
                return f
            for lv in range(1, nlev + 2):
                main.append(NL(lv))
            fi = (nlev + 1) % 2
            TTr = ["nm%d_%dt" % (h, fi) for h in HS]
            TTs = [H[h]["nm"][fi][:, 256:384] for h in HS]
            VN = [H[h]["khat"] for h in HS]
            vnr = ["khat%d" % h for h in HS]

            def S1():
                if nseg > 1:
                    for h in HS:
                        for s in range(nseg):
                            cp("pool", khm[:, h, s, 32 * s:32 * s + 32], H[h]["khatT"][:, 32 * s:32 * s + 32],
                               r=["khatT%d" % h], w=["khm%d" % h])
                            cp("pool", qm[:, h, s, 32 * s:32 * s + 32], qTs[h][:, 32 * s:32 * s + 32], r=[qr[h]],
                               w=["qm%d" % h])
                for h in HS:
                    if nseg == 1:
                        mms([(PS_[h]["B"][:, :], H[h]["khatT"][:], Sbf[:, sbase + h, :], True, True),
                             (PS_[h]["C"][:, :], qTs[h], Sbf[:, sbase + h, :], True, True)],
                            r=["khatT%d" % h, qr[h], "Sbf%d" % (sbase + h)], w=[BK_H[h]])
                    else:
                        mms([(PS_[h]["B"][:, :], khm[:, h, s, :], Sbf[:, s * 4 + h, :], s == 0, s == nseg - 1)
                             for s in range(nseg)] +
                            [(PS_[h]["C"][:, :], qm[:, h, s, :], Sbf[:, s * 4 + h, :], s == 0, s == nseg - 1)
                             for s in range(nseg)],
                            r=["khm%d" % h, "qm%d" % h] + ["Sbf%d" % (s * 4 + h) for s in range(nseg)], w=[BK_H[h]])
                    stt(H[h]["r"][:], PS_[h]["B"][:, :], negb[:, h:h + 1], H[h]["vtm"][:], ALU.mult, ALU.add,
                        r=[BK_H[h], n("negb"), "vtm%d" % h], w=["r%d" % h])
                    act(H[h]["o"][:], PS_[h]["C"][:, :], AF.Copy, scale=egb[:, h:h + 1], r=[BK_H[h], n("egb")],
                        w=["o%d" % h])
            main.append(S1)

            def S2():
                for h in HS:
                    mms([(PS_[h]["A"][:, :], TTs[h], H[h]["r"][:], True, True)], r=[TTr[h], "r%d" % h], w=[BK_H[h]])
                    if h % 2 == 0:
                        act(VN[h][:], PS_[h]["A"][:, :], AF.Copy, scale=beta[:, h:h + 1], r=[BK_H[h], n("beta")],
                            w=[vnr[h]])
                    else:
                        ts("dve", VN[h][:], PS_[h]["A"][:, :], beta[:, h:h + 1], ALU.mult, r=[BK_H[h], n("beta")],
                           w=[vnr[h]])
            main.append(S2)

            def S3():
                for h in HS:
                    si0 = h if nseg > 1 else (sbase + h)
                    if nseg == 1:
                        mms([(PS_[h]["B"][:, :], H[h]["attnT"][:], VN[h][:], True, True),
                             (PS_[h]["C"][:, :], H[h]["kdec"][:], VN[h][:], True, True)],
                            r=["attnT%d" % h, "kdec%d" % h, vnr[h]], w=[BK_H[h]])
                    else:
                        mms([(PS_[h]["B"][:, :], H[h]["attnT"][:], VN[h][:], True, True)],
                            r=["attnT%d" % h, vnr[h]], w=[BK_H[h]])
                    tt("dve", H[h]["o"][:], PS_[h]["B"][:, :], H[h]["o"][:], ALU.add, r=[BK_H[h], "o%d" % h],
                       w=["o%d" % h])
                    if nseg == 1:
                        stt(S32[:, si0, :], S32[:, si0, :], egl[:, h:h + 1], PS_[h]["C"][:, :], ALU.mult, ALU.add,
                            r=["S32_%d" % si0, n("egl"), BK_H[h]], w=["S32_%d" % si0])
                        cp("pool", Sbf[:, si0, :], S32[:, si0, :], r=["S32_%d" % si0], w=["Sbf%d" % si0])
                    act(junkh[h][:], H[h]["o"][:], AF.Square, accum=sso[:, h:h + 1], r=["o%d" % h],
                        w=["junkh%d" % h, "sso%d" % h])
                if nseg > 1:
                    for s in range(nseg):
                        for h in HS:
                            si_ = s * 4 + h
                            ts("pool", H[h]["attnT"][:], H[h]["kdec"][:], cst[:, C_SEG + s:C_SEG + s + 1], ALU.mult,
                               r=["kdec%d" % h, "cst"], w=["attnT%d" % h])
                            mms([(PS_[h]["C"][:, :], H[h]["attnT"][:], VN[h][:], True, True)],
                                r=["attnT%d" % h, vnr[h]], w=[BK_H[h]])
                            stt(S32[:, si_, :], S32[:, si_, :], egl[:, si_:si_ + 1], PS_[h]["C"][:, :], ALU.mult,
                                ALU.add, r=["S32_%d" % si_, n("egl"), BK_H[h]], w=["S32_%d" % si_])
                            cp("pool", Sbf[:, si_, :], S32[:, si_, :], r=["S32_%d" % si_], w=["Sbf%d" % si_])
            main.append(S3)

            def finA():
                stt(fb[:], sso[:], 1.0 / 128, fa[:], ALU.mult, ALU.add, r=["sso%d" % h for h in HS] + [n("fa")],
                    w=["fb"])
                act(fl[:], fb[:], AF.Ln, r=["fb"], w=["fl"])
                act(ff[:], fl[:], AF.Exp, scale=-0.5, r=["fl"], w=["ff"])
                for h in HS:
                    ts("pool", H[h]["r"][:], H[h]["o"][:], ff[:, h:h + 1], ALU.mult, r=["o%d" % h, "ff"],
                       w=["r%d" % h])

            def finB():
                for h in HS:
                    trs([(PS_[h]["TR"][:, 0:128], H[h]["r"][:])], r=["r%d" % h], w=[BK_H[h]])
                    stt(mx[:, 4 + h, tsl], PS_[h]["TR"][:, 0:128], gnw_t[:, 0:1], sz[:, h, tsl], ALU.mult, ALU.mult,
                        r=[BK_H[h], "gnw", "szg%d_%d" % (pb, h)], w=["mixT%d_%d" % (pb, 4 + h)])
            return dict(prepA=prepA, prepB=[prepB(0), prepB(1)], main=main, finA=finA, finB=finB)

        pOb = pT[:].rearrange("p a b -> p (a b)").bitcast(F32)

        def d_units(bi, j, dst_ap):
            buf = bi % 2
            mx = mixT[buf]
            tsl = slice(j * 128, (j + 1) * 128)
            yt = ytmp[j % 2]
            yr = "ytmp%d" % (j % 2)

            def half(hf):
                def f():
                    mms([(pOb[:, :], mx[:, k, tsl], w_out_bf[:, k, hf * 512:(hf + 1) * 512], k == 0, k == 7)
                         for k in range(8)], r=MIX_ALL(buf) + ["w_out_bf"], w=[BK_T])
                    act(junk[:, 0:512], pOb[:, :], AF.Square, accum=ss2[:, hf:hf + 1], r=[BK_T],
                        w=["junk", "ss2_%d" % hf])
                    cp("dve", yt[:, hf * 512:(hf + 1) * 512], pOb[:, :], r=[BK_T], w=[yr])
                return f

            def tail():
                tt("dve", ss2s[:], ss2[:, 0:1], ss2[:, 1:2], ALU.add, r=["ss2_0", "ss2_1"], w=["ss2s"])
                act(lt2[:], ss2s[:], AF.Ln, scale=1.0 / D, bias=1e-6, r=["ss2s"], w=["lt2"])
                act(rstd2[:], lt2[:], AF.Exp, scale=-0.5, r=["lt2"], w=["rstd2"])
                stt(yt[:], yt[:], rstd2[:, 0:1], lnpost_bc[:], ALU.mult, ALU.mult, r=[yr, "rstd2", "lnpost"], w=[yr])
                tt("pool", yt[:], yt[:], xt[buf][:, j, :], ALU.add, r=[yr, "xt%d" % buf], w=[yr])
                dma(dst_ap[j * 128:(j + 1) * 128, :], yt[:], r=[yr], final=True)
            return [half(0), half(1), tail]

        blocks = []
        for sq in range(n_pseq):
            for b in range(nblk):
                blocks.append(dict(kind="p", sq=sq, b=b, ntile=2, ntok=NT, nseg=1, nlev=6,
                                   src=xp[sq, b * NT:(b + 1) * NT, :], dst=yp[sq, b * NT:(b + 1) * NT, :],
                                   first=(b == 0), last=(b == nblk - 1)))
        if n_sseq:
            blocks.append(dict(kind="s", ntile=1, ntok=128, nseg=4, nlev=4, src=xs, dst=ys, first=True, last=True))

        def x_stream(bi):
            blk = blocks[bi]
            if blk["kind"] == "p":
                if blk["first"]:
                    memset("pool", ua[:], 0.0, w=UA_ALL)
                    memset("pool", pre[:], 0.0, w=PRE_ALL)
            else:
                ua_s = ua[:, 0:4 * 4 * (2 + 32)].rearrange("p (c s l) -> p c s l", c=4, s=4)
                pre_s = pre[:, 0:12 * 4 * (3 + 32)].rearrange("p (c s l) -> p c s l", c=12, s=4)
                for s in range(4):
                    for t in range(2):
                        dma(ua_s[:, :, s, t], cca[s, t].rearrange("(c p) -> p c", p=128), w=UA_ALL, nonc=True)
                    for t in range(3):
                        dma(pre_s[:, :, s, t], ccq[s, t].rearrange("(c p) -> p c", p=128), w=PRE_ALL, nonc=True)
            yield from gen_a(bi, blk["ntile"])
            yield from gen_b(bi, blk["ntok"], blk["nseg"])
            conv_tail(blk, blk["nseg"], blk["ntok"] // blk["nseg"])
            yield

        def y_stream(bi):
            blk = blocks[bi]
            if blk["kind"] == "p":
                if blk["first"]:
                    for h in range(4):
                        memset("pool", S32[:, h, :], 0.0, w=["S32_%d" % h])
                        memset("pool", Sbf[:, h, :], 0.0, w=["Sbf%d" % h])
            else:
                for s in range(4):
                    dma(S32[:, s * 4:(s + 1) * 4, :], sg[s].rearrange("h k v -> k h v"),
                        w=["S32_%d" % (s * 4 + h) for h in range(4)])
                for i in range(16):
                    cp("pool", Sbf[:, i, :], S32[:, i, :], r=["S32_%d" % i], w=["Sbf%d" % i])
            nt_ = blk["ntile"]
            TU = [tile_units(bi, j, blk["nseg"], 0, blk["nlev"]) for j in range(nt_)]
            DU = [d_units(bi, j, blk["dst"]) for j in range(nt_)]
            seq = []
            for j in range(nt_):
                seq.append(TU[j]["prepA"])
            seq += TU[0]["prepB"]
            if nt_ == 1:
                seq += TU[0]["main"] + [TU[0]["finA"], TU[0]["finB"]] + DU[0]
            else:
                m0, m1 = TU[0]["main"], TU[1]["main"]
                seq += m0[0:6] + [TU[1]["prepB"][0]] + m0[6:7] + [TU[1]["prepB"][1]] + m0[7:]
                seq += [TU[0]["finA"]] + m1[0:2] + [TU[0]["finB"]] + m1[2:6]
                rest = m1[6:]
                seq += rest[0:1] + DU[0][0:1] + rest[1:2] + DU[0][1:2] + rest[2:3] + DU[0][2:3] + rest[3:]
                seq += [TU[1]["finA"], TU[1]["finB"]] + DU[1]
            d_all = [u for du in DU for u in du]
            neu = [u for j in range(nt_) for u in TU[j]["main"][5:5 + blk["nlev"] + 1]]
            wts = [(W_NEU if any(u is x_ for x_ in neu) else 1.0) for u in seq]
            wtot = sum(wts)
            acc = 0.0
            for k, u in enumerate(seq):
                yield (bi, any(u is d_ for d_ in d_all), acc / wtot)
                acc += wts[k]
                u()
                if blk["last"] and u is TU[nt_ - 1]["main"][-1]:
                    if blk["kind"] == "p":
                        dma(o_gp[blk["sq"]].rearrange("h k v -> k h v"), S32[:, 0:4, :],
                            r=["S32_%d" % h for h in range(4)], final=True)
                    else:
                        for s in range(4):
                            dma(o_gs[s].rearrange("h k v -> k h v"), S32[:, s * 4:(s + 1) * 4, :],
                                r=["S32_%d" % (s * 4 + h) for h in range(4)], final=True)

        NXU, LEAD, PREFETCH_AT, X_PARTIAL, W_NEU = 43.0, 0.75, 34, 26, 0.45
        nb = len(blocks)
        xunits = [0] * nb
        xfull = [False] * nb

        x_need = [-1]

        def x_all():
            load_x(0, blocks[0]["src"], blocks[0]["ntile"])
            for bi_ in range(nb):
                k = 0
                pref = False
                for _ in x_stream(bi_):
                    k += 1
                    xunits[bi_] = k
                    if k == PREFETCH_AT and bi_ + 1 < nb:
                        load_x(bi_ + 1, blocks[bi_ + 1]["src"], blocks[bi_ + 1]["ntile"])
                        pref = True
                    x_need[0] = (bi_ - 1) if (k == PREFETCH_AT - 1) else -1
                    yield bi_ + min(k / NXU, 0.999)
                xfull[bi_] = True
                x_need[0] = bi_ - 1
                yield bi_ + 1.0
                if not pref and bi_ + 1 < nb:
                    load_x(bi_ + 1, blocks[bi_ + 1]["src"], blocks[bi_ + 1]["ntile"])

        def y_all():
            for bi_ in range(nb):
                for (b_, full_, frac_) in y_stream(bi_):
                    yield (bi_ + min(frac_, 0.999), b_, full_)

        xg, yg = x_all(), y_all()
        xpos = 0.0
        xdone = False
        _end = object()
        ynext = next(yg, _end)

        def y_complete(b_):
            return b_ < 0 or ynext is _end or ynext[1] > b_

        while ynext is not _end or not xdone:
            if ynext is not _end:
                ypos, yb, yfull = ynext
                sat = xfull[yb] or ((not yfull) and xunits[yb] >= X_PARTIAL)
            else:
                sat = False
            x_ok = (not xdone) and y_complete(x_need[0])
            if sat and (not x_ok or ypos + LEAD <= xpos):
                ynext = next(yg, _end)
            elif x_ok:
                v = next(xg, _end)
                if v is _end:
                    xdone = True
                else:
                    xpos = v
            else:
                raise AssertionError("pipeline driver deadlock")
        S.emit(final_wait_ops=final_ops)
    return nc


_CACHE = {}
SB_BYTES = [0]


def _get_program(key):
    if key not in _CACHE:
        _CACHE[key] = build_program(*key)
    return _CACHE[key]


def make_in_maps(x_prompt, x_sample, cache_conv_a, cache_conv_qkv, state_gdn, ln_pre, w_in, conv_a_w,
                 conv_qkv_w, a_log, dt_bias, gdn_norm_w, w_out, ln_post, n_cores, n_pseq, n_sseq):
    f = lambda a: np.ascontiguousarray(np.asarray(a, dtype=np.float32))
    cst = make_consts()
    maps = []
    for c in range(n_cores):
        maps.append({
            "xp": f(x_prompt[c * n_pseq:(c + 1) * n_pseq]),
            "xs": f(x_sample[c * n_sseq:(c + 1) * n_sseq]).reshape(-1, D),
            "cca": f(cache_conv_a[0, c * n_sseq:(c + 1) * n_sseq]),
            "ccq": f(cache_conv_qkv[0, c * n_sseq:(c + 1) * n_sseq]),
            "sg": f(state_gdn[0, c * n_sseq:(c + 1) * n_sseq]),
            "ln_pre": f(ln_pre[0]), "w_in": f(w_in[0]), "caw": f(conv_a_w[0]), "cqw": f(conv_qkv_w[0]),
            "a_log": f(a_log[0]), "dt_bias": f(dt_bias[0]), "gnw": f(gdn_norm_w[0]),
            "w_out": f(w_out[0]), "ln_post": f(ln_post[0]), "cst": cst,
        })
    return maps


def kernel(x_prompt, x_sample, cache_conv_a, cache_conv_qkv, state_gdn, ln_pre, w_in, conv_a_w,
           conv_qkv_w, a_log, dt_bias, gdn_norm_w, w_out, ln_post):
    x_prompt = np.asarray(x_prompt)
    x_sample = np.asarray(x_sample)
    B, T, _ = x_prompt.shape
    Bs, Ts, _ = x_sample.shape
    n_pseq = B // NCORES
    n_sseq = Bs // NCORES
    nc = _get_program((n_pseq, T, n_sseq, Ts))
    maps = make_in_maps(x_prompt, x_sample, cache_conv_a, cache_conv_qkv, state_gdn, ln_pre, w_in, conv_a_w,
                        conv_qkv_w, a_log, dt_bias, gdn_norm_w, w_out, ln_post, NCORES, n_pseq, n_sseq)
    res = run_bass_kernel_spmd(nc, maps, core_ids=list(range(NCORES))).results
    cat = lambda k: np.concatenate([np.asarray(r[k]) for r in res], axis=0)
    y_p = cat("yp")
    y_s = cat("ys").reshape(Bs, Ts, D)
    return (y_p, y_s, cat("o_cap")[None], cat("o_cqp")[None], cat("o_gp")[None],
            cat("o_cas")[None], cat("o_cqs")[None], cat("o_gs")[None])
```

```python
from contextlib import ExitStack

import numpy as np
import concourse.bass as bass
import concourse.mybir as mybir
from concourse.bass_utils import run_bass_kernel_spmd

F32 = mybir.dt.float32
BF16 = mybir.dt.bfloat16
AF = mybir.ActivationFunctionType
ALU = mybir.AluOpType

ENGS = ("pe", "act", "dve", "pool", "sp")

SB_BYTES = [0]
D = 1024
DP = 4104
NCORES = 8


class Op:
    __slots__ = ("eng", "fn", "reads", "writes", "deps", "signal", "count",
                 "is_dma", "dsem", "dval", "dprev", "idx")

    def __init__(self, eng, fn, reads, writes, is_dma):
        self.eng = eng
        self.fn = fn
        self.reads = reads
        self.writes = writes
        self.deps = []
        self.signal = False
        self.count = 0
        self.is_dma = is_dma
        self.dsem = None
        self.dval = 0
        self.dprev = 0
        self.idx = -1


class Sched:
    def __init__(self, nc, n_dma_sems=24):
        self.nc = nc
        self.ops = []
        self.last_w = {}
        self.readers = {}
        self.last_acc = {}
        self.n_dma_sems = n_dma_sems

    def op(self, eng, fn, reads=(), writes=(), dma=False):
        o = Op(eng, fn, tuple(reads), tuple(writes), dma)
        o.idx = len(self.ops)
        deps = {}
        banks = set(r for r in (o.reads + o.writes) if r.startswith("B:"))
        o.reads = tuple(r for r in o.reads if not r.startswith("B:"))
        o.writes = tuple(r for r in o.writes if not r.startswith("B:"))
        for bk in banks:
            la = self.last_acc.get(bk)
            if la is not None and not (la.eng == "pe" and eng == "pe"):
                deps[la.idx] = la
            self.last_acc[bk] = o
        for r in o.reads:
            w = self.last_w.get(r)
            if w is not None:
                deps[w.idx] = w
        for r in o.writes:
            w = self.last_w.get(r)
            if w is not None:
                deps[w.idx] = w
            for rd in self.readers.get(r, ()):
                deps[rd.idx] = rd
        for r in o.reads:
            self.readers.setdefault(r, []).append(o)
        for r in o.writes:
            self.last_w[r] = o
            self.readers[r] = []
        deps.pop(o.idx, None)
        o.deps = list(deps.values())
        for d in o.deps:
            d.signal = True
        self.ops.append(o)
        return o

    def emit(self, final_wait_ops=()):
        nc = self.nc
        cnt = {e: 0 for e in ENGS}
        dma_use = [0] * self.n_dma_sems
        dma_rr = 0
        for o in self.ops:
            if o.is_dma:
                s = dma_rr % self.n_dma_sems
                dma_rr += 1
                o.dsem = s
                o.dprev = dma_use[s] * 16
                dma_use[s] += 1
                o.dval = dma_use[s] * 16
            elif o.signal:
                cnt[o.eng] += 1
                o.count = cnt[o.eng]
        with ExitStack() as es:
            esem = {e: es.enter_context(nc.semaphore("s_" + e)) for e in ENGS if e != "sp"}
            dsems = [es.enter_context(nc.semaphore("d%d" % i)) for i in range(self.n_dma_sems)]
            block = es.enter_context(nc.Block())
            per_eng = {e: [o for o in self.ops if o.eng == e] for e in ENGS}

            def run(ename, eng):
                waited = {}

                def need(key, sem, val):
                    if waited.get(key, 0) >= val:
                        return
                    waited[key] = val
                    eng.wait_ge(sem, val)

                for o in per_eng[ename]:
                    for d in o.deps:
                        if d.is_dma:
                            need(("d", d.dsem), dsems[d.dsem], d.dval)
                        else:
                            need(d.eng, esem[d.eng], d.count)
                    if o.is_dma:
                        if o.dprev:
                            need(("d", o.dsem), dsems[o.dsem], o.dprev)
                        ins = o.fn(eng)
                        ins.then_inc(dsems[o.dsem], 16)
                    else:
                        ins = o.fn(eng)
                        if o.signal:
                            ins.then_inc(esem[ename], 1)
                if ename == "sp":
                    for o in final_wait_ops:
                        need(("d", o.dsem), dsems[o.dsem], o.dval)

            @block.tensor
            def _(e):
                run("pe", e)

            @block.scalar
            def _(e):
                run("act", e)

            @block.vector
            def _(e):
                run("dve", e)

            @block.gpsimd
            def _(e):
                run("pool", e)

            @block.sync
            def _(e):
                run("sp", e)


C_ID, C_ONE, C_PMI, C_PMS, C_PSL, C_SMI, C_SMS, C_SSL, C_SEG = (
    0, 128, 256, 384, 512, 640, 768, 896, 1024)
C_TOT = 1028


def make_consts():
    c = np.zeros((128, C_TOT), np.float32)
    idx = np.arange(128)
    c[:, C_ID:C_ID + 128] = np.eye(128)
    c[:, C_ONE:C_ONE + 128] = 1.0
    for base, segsz in ((C_PMI, 128), (C_SMI, 32)):
        same = (idx[:, None] // segsz) == (idx[None, :] // segsz)
        c[:, base:base + 128] = ((idx[None, :] >= idx[:, None]) & same)
        c[:, base + 128:base + 256] = ((idx[None, :] > idx[:, None]) & same)
        c[:, base + 256:base + 384] = ((idx[:, None] > idx[None, :]) & same)
    for s in range(4):
        c[:, C_SEG + s] = (idx // 32 == s)
    return c


def build_program(n_pseq=2, seq_len=4096, n_sseq=4, s_len=32):
    NT = 256
    nblk = seq_len // NT
    assert seq_len % NT == 0 and n_sseq * s_len == 128
    nc = bass.Bass("TRN2", target_bir_lowering=False)

    def din(name, shape):
        return nc.dram_tensor(name, list(shape), F32, kind="ExternalInput").ap()

    def dout(name, shape):
        return nc.dram_tensor(name, list(shape), F32, kind="ExternalOutput").ap()

    xp = din("xp", [n_pseq, seq_len, D])
    xs = din("xs", [n_sseq * s_len, D])
    cca = din("cca", [n_sseq, 2, 512])
    ccq = din("ccq", [n_sseq, 3, 1536])
    sg = din("sg", [n_sseq, 4, 128, 128])
    ln_pre = din("ln_pre", [D])
    w_in = din("w_in", [D, DP])
    caw = din("caw", [3, 512])
    cqw = din("cqw", [4, 1536])
    a_log = din("a_log", [4])
    dt_bias = din("dt_bias", [4])
    gnw_d = din("gnw", [128])
    w_out = din("w_out", [D, D])
    ln_post = din("ln_post", [D])
    cst_d = din("cst", [128, C_TOT])

    yp = dout("yp", [n_pseq, seq_len, D])
    ys = dout("ys", [n_sseq * s_len, D])
    o_cap = dout("o_cap", [n_pseq, 2, 512])
    o_cqp = dout("o_cqp", [n_pseq, 3, 1536])
    o_gp = dout("o_gp", [n_pseq, 4, 128, 128])
    o_cas = dout("o_cas", [n_sseq, 2, 512])
    o_cqs = dout("o_cqs", [n_sseq, 3, 1536])
    o_gs = dout("o_gs", [n_sseq, 4, 128, 128])

    es = ExitStack()
    with es:
        def sb(name, shape, dt=F32):
            n = 1
            for d_ in shape[1:]:
                n *= d_
            SB_BYTES[0] += n * (2 if dt == BF16 else 4)
            return es.enter_context(nc.sbuf_tensor("sb_" + name, list(shape), dt))

        def ps(name, shape, dt=F32):
            return es.enter_context(nc.psum_tensor("ps_" + name, list(shape), dt))

        S = Sched(nc)
        final_ops = []

        w_in_bf = sb("w_in_bf", [128, 8, DP], BF16)
        w_out_bf = sb("w_out_bf", [128, 8, D], BF16)
        cst = sb("cst", [128, C_TOT])
        ident_bf = sb("ident_bf", [128, 128], BF16)
        ones_bf = sb("ones_bf", [128, 2], BF16)
        lnpre_t = sb("lnpre_t", [128, 8])
        caw_t = sb("caw_t", [128, 4, 3])
        cqw_t = sb("cqw_t", [128, 12, 4])
        gnw_t = sb("gnw_t", [128, 1])
        lnpost_bc = sb("lnpost_bc", [128, D])
        dtb_bc = sb("dtb_bc", [128, 4])
        negA_bc = sb("negA_bc", [128, 4])
        xt = [sb("xt%d" % i, [128, 2, D]) for i in range(2)]
        junk = sb("junk", [128, 512], BF16)
        hb = sb("hb", [128, D], BF16)
        hT = sb("hT", [128, 8, NT], BF16)
        ss1 = sb("ss1", [128, 2])
        lt1 = sb("lt1", [128, 2])
        rstd1 = sb("rstd1", [128, 2])
        ca_sb = sb("ca_sb", [128, NT])
        ua = sb("ua", [128, 4 * (2 + NT)])
        acc_a = sb("acc_a", [128, NT])
        sza = sb("sza", [128, NT])
        pre = sb("pre", [128, 12 * (3 + NT)])
        accq = [sb("accq%d" % i, [128, NT]) for i in range(3)]
        qkvT = [sb("qkvT%d" % i, [128, 12, NT], BF16) for i in range(2)]
        szg = [sb("szg%d" % i, [128, 4, NT], BF16) for i in range(2)]
        mixT = [sb("mixT%d" % i, [128, 8, NT], BF16) for i in range(2)]
        ab_sb = [sb("ab_sb%d" % i, [128, 2, 8]) for i in range(2)]
        TINY = []
        for i in range(2):
            TINY.append(dict(
                abx=sb("abx%d" % i, [128, 4]), e1=sb("e1%d" % i, [128, 4]), sp_t=sb("sp_t%d" % i, [128, 4]),
                g_t=sb("g_t%d" % i, [128, 4]), gseg=sb("gseg%d" % i, [128, 16]), e2=sb("e2%d" % i, [128, 4]),
                nbeta=sb("nbeta%d" % i, [128, 4]), beta=sb("beta%d" % i, [128, 4]), egb=sb("egb%d" % i, [128, 4]),
                negb=sb("negb%d" % i, [128, 4]), ekd=sb("ekd%d" % i, [128, 4]), egl=sb("egl%d" % i, [128, 16])))
        gUs = sb("gUs", [128, 4, 128])
        EG = sb("EG", [128, 4, 128])
        ssk = sb("ssk", [128, 4])
        lk = sb("lk", [128, 4])
        rk = sb("rk", [128, 4])
        sso = sb("sso", [128, 4])
        fa = sb("fa", [128, 4])
        fb = sb("fb", [128, 4])
        fl = sb("fl", [128, 4])
        ff = sb("ff", [128, 4])
        H = []
        for h in range(4):
            d = dict(
                khat=sb("khat%d" % h, [128, 128], BF16),
                vtm=sb("vtm%d" % h, [128, 128], BF16),
                khatT=sb("khatT%d" % h, [128, 128], BF16),
                qsq=sb("qsq%d" % h, [128, 128], BF16),
                nm=[sb("nm%d_%d" % (h, i), [128, 384], BF16) for i in range(2)],
                attnT=sb("attnT%d" % h, [128, 128], BF16),
                r=sb("r%d" % h, [128, 128], BF16),
                kdec=sb("kdec%d" % h, [128, 128], BF16),
                o=sb("o%d" % h, [128, 128]),
            )
            H.append(d)
        khm = sb("khm", [128, 4, 4, 128], BF16)
        qm = sb("qm", [128, 4, 4, 128], BF16)
        junkh = [sb("junkh%d" % i, [128, 128], BF16) for i in range(4)]
        S32 = sb("S32", [128, 16, 128])
        Sbf = sb("Sbf", [128, 16, 128], BF16)
        ss2 = sb("ss2", [128, 2])
        ss2s = sb("ss2s", [128, 1])
        lt2 = sb("lt2", [128, 1])
        rstd2 = sb("rstd2", [128, 1])
        ytmp = [sb("ytmp%d" % i, [128, D]) for i in range(2)]

        pT = ps("pT", [128, 8, 128], BF16)
        pP = [ps("pP%d" % i, [128, 512]) for i in range(2)]
        pO = pP
        pD = ps("pD", [128, 512])
        pHb = [ps("pH%d" % i, [128, 512]) for i in range(4)]
        BK_T, BK_D = "B:T", "B:D"
        BK_P = ["B:P0", "B:P1"]
        BK_H = ["B:H%d" % i for i in range(4)]
        pTiny = pD[:, 256:384]
        R_TINY = BK_D

        def cs(off, n=128):
            return cst[:, off:off + n]

        def dma(out, in_, r=(), w=(), nonc=False, final=False):
            if nonc:
                o = S.op("sp", lambda e: e.dma_start(out=out, in_=in_, allow_slow_non_contiguous=True), r, w, dma=True)
            else:
                o = S.op("sp", lambda e: e.dma_start(out=out, in_=in_), r, w, dma=True)
            if final:
                final_ops.append(o)
            return o

        def ts(eng, out, in0, s1, op0, s2=None, op1=None, r=(), w=()):
            if op1 is None and eng == "pool" and op0 == ALU.mult:
                s2, op1 = 0.0, ALU.add
            if op1 is None:
                return S.op(eng, lambda e: e.tensor_scalar(out=out, in0=in0, scalar1=s1, scalar2=None, op0=op0), r, w)
            return S.op(eng, lambda e: e.tensor_scalar(out=out, in0=in0, scalar1=s1, scalar2=s2, op0=op0, op1=op1), r, w)

        def tt(eng, out, in0, in1, op, r=(), w=()):
            return S.op(eng, lambda e: e.tensor_tensor(out=out, in0=in0, in1=in1, op=op), r, w)

        def stt(out, in0, scalar, in1, op0, op1, r=(), w=()):
            return S.op("dve", lambda e: e.scalar_tensor_tensor(out=out, in0=in0, scalar=scalar, in1=in1, op0=op0, op1=op1), r, w)

        def act(out, in_, func, scale=None, bias=None, accum=None, r=(), w=()):
            kw = {}
            if scale is not None:
                kw["scale"] = scale
            if bias is not None:
                kw["bias"] = bias
            if accum is not None:
                kw["accum_out"] = accum
            return S.op("act", lambda e: e.activation(out=out, in_=in_, func=func, **kw), r, w)

        def cp(eng, out, in_, r=(), w=()):
            if eng == "act":
                return S.op("act", lambda e: e.copy(out=out, in_=in_), r, w)
            return S.op(eng, lambda e: e.tensor_copy(out=out, in_=in_), r, w)

        def mms(lst, r=(), w=()):
            def f(e):
                ins = None
                for (out, lhsT, rhs, st, sp) in lst:
                    ins = e.matmul(out, lhsT=lhsT, rhs=rhs, start=st, stop=sp)
                return ins
            return S.op("pe", f, r, w)

        def trs(lst, r=(), w=()):
            def f(e):
                ins = None
                for (out, in_) in lst:
                    ins = e.transpose(out=out, in_=in_, identity=ident_bf[:])
                return ins
            return S.op("pe", f, tuple(r) + ("ident_bf",), w)

        def memset(eng, ap, val, w=()):
            return S.op(eng, lambda e: e.memset(ap, val), (), w)

        dma(cst[:], cst_d, w=["cst"])
        dma(lnpre_t[:], ln_pre.rearrange("(k p) -> p k", p=128), w=["lnpre"], nonc=True)
        for j in range(3):
            dma(caw_t[:, :, j], caw[j].rearrange("(c p) -> p c", p=128), w=["caw"], nonc=True)
        for j in range(4):
            dma(cqw_t[:, :, j], cqw[j].rearrange("(c p) -> p c", p=128), w=["cqw"], nonc=True)
        dma(gnw_t[:], gnw_d.rearrange("(p o) -> p o", o=1), w=["gnw"], nonc=True)
        dma(lnpost_bc[:], ln_post.partition_broadcast(128), w=["lnpost"])
        dma(dtb_bc[:], dt_bias.partition_broadcast(128), w=["dtb"])
        dma(negA_bc[:], a_log.partition_broadcast(128), w=["negA"])
        act(negA_bc[:], negA_bc[:], AF.Exp, r=["negA"], w=["negA"])
        ts("dve", negA_bc[:], negA_bc[:], -1.0, ALU.mult, r=["negA"], w=["negA"])
        cp("dve", ident_bf[:], cs(C_ID), r=["cst"], w=["ident_bf"])
        memset("dve", ones_bf[:], 1.0, w=["ones_bf"])
        memset("pool", khm[:], 0.0, w=["khm%d" % h for h in range(4)])
        memset("pool", qm[:], 0.0, w=["qm%d" % h for h in range(4)])
        PIECE = 1024
        f32v = lambda t, pat: t[:].rearrange(pat).bitcast(F32)
        stg = [(ytmp[0][:], ["ytmp0"]), (ytmp[1][:], ["ytmp1"]),
               (f32v(mixT[1], "p a b -> p (a b)"), ["mixT1_%d" % i for i in range(8)]),
               (S32[:, 0:8, :].rearrange("p a b -> p (a b)"), ["S32_%d" % i for i in range(8)]),
               (S32[:, 8:16, :].rearrange("p a b -> p (a b)"), ["S32_%d" % i for i in range(8, 16)]),
               (xt[1][:, 0, :], ["xt1"]),
               (f32v(qkvT[1], "p a b -> p (a b)")[:, 0:PIECE], ["qkvT1_%d" % i for i in range(12)])]
        abst = junk[:, 0:128].bitcast(F32).rearrange("p (k c) -> p k c", k=8)
        dma(abst, w_in[:, 4096:4104].rearrange("(k p) c -> p k c", p=128), w=["junk"], nonc=True)
        for k in range(8):
            ts("dve", w_in_bf[:, k, 4096:4104], abst[:, k, :], lnpre_t[:, k:k + 1], ALU.mult, r=["junk", "lnpre"],
               w=["w_in_bf_ab"])
        si = 0
        for pi in (2, 3, 0, 1):
            for k in range(8):
                c0 = pi * PIECE
                sv, rn = stg[si % len(stg)]
                stv = sv[:, 0:PIECE]
                dma(stv, w_in[k * 128:(k + 1) * 128, c0:c0 + PIECE], w=rn)
                eng = ("dve", "pool")[si % 2]
                ts(eng, w_in_bf[:, k, c0:c0 + PIECE], stv, lnpre_t[:, k:k + 1], ALU.mult,
                   r=rn + ["lnpre"], w=["w_in_bf_p%d" % pi])
                si += 1
        for k in range(8):
            sv, rn = stg[si % len(stg)]
            stv = sv[:, 0:D]
            dma(stv, w_out[k * 128:(k + 1) * 128, :], w=rn)
            eng = ("dve", "pool")[si % 2]
            cp(eng, w_out_bf[:, k, :], stv, r=rn, w=["w_out_bf"])
            si += 1

        def wres(c0, c1):
            if c0 >= 4096:
                return ["w_in_bf_ab"]
            return ["w_in_bf_p%d" % p for p in range(c0 // PIECE, (c1 - 1) // PIECE + 1)]

        UA_ALL = ["ua%d" % i for i in range(4)]
        PRE_ALL = ["pre%d" % i for i in range(12)]

        def MIX_ALL(pb):
            return ["mixT%d_%d" % (pb, i) for i in range(8)]

        def load_x(bi, src_ap, ntile):
            buf = bi % 2
            dma(xt[buf][:, 0:ntile, :], src_ap.rearrange("(j p) d -> p j d", p=128), w=["xt%d" % buf])

        def gen_a(bi, ntile):
            buf = bi % 2
            xr = "xt%d" % buf
            for j in range(ntile):
                act(hb[:], xt[buf][:, j, :], AF.Square, accum=ss1[:, j:j + 1], r=[xr], w=["hb", "ss1"])
            act(lt1[:, 0:ntile], ss1[:, 0:ntile], AF.Ln, scale=1.0 / D, bias=1e-6, r=["ss1"], w=["lt1"])
            act(rstd1[:, 0:ntile], lt1[:, 0:ntile], AF.Exp, scale=-0.5, r=["lt1"], w=["rstd1"])
            yield
            for j in range(ntile):
                ts("dve", hb[:], xt[buf][:, j, :], rstd1[:, j:j + 1], ALU.mult, r=[xr, "rstd1"], w=["hb"])
                trs([(pT[:, k, :], hb[:, k * 128:(k + 1) * 128]) for k in range(8)], r=["hb"], w=[BK_T])
                cp("act", hT[:, :, j * 128:(j + 1) * 128], pT[:], r=[BK_T], w=["hT"])
                yield

        ppi = [0]

        def proj_chunk(c, ntok):
            i = ppi[0] % 2
            ppi[0] += 1
            mms([(pP[i][:, 0:ntok], w_in_bf[:, k, c * 128:(c + 1) * 128], hT[:, k, 0:ntok], k == 0, k == 7)
                 for k in range(8)], r=wres(c * 128, c * 128 + 128) + ["hT"], w=[BK_P[i]])
            return pP[i], BK_P[i]

        aqi = [0]

        def gen_b(bi, ntok, nseg):
            pb = bi % 2
            ntile = ntok // 128
            L = ntok // nseg
            ua_v = ua[:, 0:4 * nseg * (2 + L)].rearrange("p (c s l) -> p c s l", c=4, s=nseg)
            pre_v = pre[:, 0:12 * nseg * (3 + L)].rearrange("p (c s l) -> p c s l", c=12, s=nseg)
            qk, sz, mx = qkvT[pb], szg[pb], mixT[pb]

            def v3(ap2):
                return ap2.rearrange("p (s l) -> p s l", s=nseg)

            jobs = []

            def mm_stage(c):
                def f(bk):
                    mms([(pP[bk][:, 0:ntok], w_in_bf[:, k, c * 128:(c + 1) * 128], hT[:, k, 0:ntok], k == 0, k == 7)
                         for k in range(8)], r=wres(c * 128, c * 128 + 128) + ["hT"], w=[BK_P[bk]])
                return f

            for j in range(ntile):
                def s0(bk, j=j):
                    mms([(pP[bk][:, 0:8], hT[:, k, j * 128:(j + 1) * 128], w_in_bf[:, k, 4096:4104], k == 0, k == 7)
                         for k in range(8)], r=["hT"] + wres(4096, 4104), w=[BK_P[bk]])

                def s1(bk, j=j):
                    cp("dve", ab_sb[pb][:, j, :], pP[bk][:, 0:8], r=[BK_P[bk]], w=["ab%d_%d" % (pb, j)])
                jobs.append([s0, s1])
            for c in list(range(4, 8)) + list(range(8, 12)) + list(range(0, 4)):
                ai = aqi[0] % 3
                aqi[0] += 1
                aq, ar = accq[ai], "accq%d" % ai

                def s1(bk, c=c):
                    cp("act", pre_v[:, c, :, 3:3 + L], v3(pP[bk][:, 0:ntok]), r=[BK_P[bk]], w=["pre%d" % c])

                def s2(bk, c=c, aq=aq, ar=ar):
                    ts("pool", v3(aq[:, 0:ntok]), pre_v[:, c, :, 0:L], cqw_t[:, c, 0:1], ALU.mult,
                       r=["pre%d" % c, "cqw"], w=[ar])

                def s3(bk, c=c, aq=aq, ar=ar):
                    for j in range(1, 4):
                        stt(v3(aq[:, 0:ntok]), pre_v[:, c, :, j:j + L], cqw_t[:, c, j:j + 1], v3(aq[:, 0:ntok]),
                            ALU.mult, ALU.add, r=["pre%d" % c, "cqw", ar], w=[ar])

                def s4(bk, c=c, aq=aq, ar=ar):
                    act(qk[:, c, 0:ntok], aq[:, 0:ntok], AF.Silu, r=[ar], w=["qkvT%d_%d" % (pb, c)])
                jobs.append([mm_stage(16 + c), s1, s2, s3, s4])
            for c in range(4):
                def s1(bk, c=c):
                    act(sz[:, c, 0:ntok], pP[bk][:, 0:ntok], AF.Silu, r=[BK_P[bk]], w=["szg%d_%d" % (pb, c)])
                jobs.append([mm_stage(28 + c), s1])
            for cg in range(4):
                def c1(bk):
                    cp("act", ca_sb[:, 0:ntok], pP[bk][:, 0:ntok], r=[BK_P[bk]], w=["ca_sb"])
                jobs.append([mm_stage(4 + cg), c1])

                def x1(bk, cg=cg):
                    tt("dve", ua_v[:, cg, :, 2:2 + L], v3(pP[bk][:, 0:ntok]), v3(ca_sb[:, 0:ntok]), ALU.mult,
                       r=[BK_P[bk], "ca_sb"], w=["ua%d" % cg])

                def x2(bk, cg=cg):
                    ts("pool", v3(acc_a[:, 0:ntok]), ua_v[:, cg, :, 0:L], caw_t[:, cg, 0:1], ALU.mult,
                       r=["ua%d" % cg, "caw"], w=["acc_a"])

                def x3(bk, cg=cg):
                    for j in range(1, 3):
                        stt(v3(acc_a[:, 0:ntok]), ua_v[:, cg, :, j:j + L], caw_t[:, cg, j:j + 1],
                            v3(acc_a[:, 0:ntok]), ALU.mult, ALU.add, r=["ua%d" % cg, "caw", "acc_a"], w=["acc_a"])
                jobs.append([mm_stage(8 + cg), x1, x2, x3])

                def z1(bk):
                    act(sza[:, 0:ntok], pP[bk][:, 0:ntok], AF.Silu, r=[BK_P[bk]], w=["sza"])
                jobs.append([mm_stage(12 + cg), z1])

                def b1(bk):
                    tt("dve", acc_a[:, 0:ntok], pP[bk][:, 0:ntok], acc_a[:, 0:ntok], ALU.mult, r=[BK_P[bk], "acc_a"],
                       w=["acc_a"])

                def b2(bk, cg=cg):
                    tt("pool", mx[:, cg, 0:ntok], acc_a[:, 0:ntok], sza[:, 0:ntok], ALU.mult,
                       r=["acc_a", "sza"], w=["mixT%d_%d" % (pb, cg)])
                jobs.append([mm_stage(cg), b1, b2])
            nj = len(jobs)
            base = ppi[0]
            ppi[0] += nj
            for u in range(nj + 4):
                for st in (4, 3, 2, 1, 0):
                    ji = u - st
                    if 0 <= ji < nj and st < len(jobs[ji]) and jobs[ji][st] is not None:
                        jobs[ji][st]((base + ji) % 2)
                yield

        def conv_tail(blk, nseg, L):
            ua_v = ua[:, 0:4 * nseg * (2 + L)].rearrange("p (c s l) -> p c s l", c=4, s=nseg)
            pre_v = pre[:, 0:12 * nseg * (3 + L)].rearrange("p (c s l) -> p c s l", c=12, s=nseg)
            if blk["kind"] == "p":
                if blk["last"]:
                    sq = blk["sq"]
                    for t in range(2):
                        dma(o_cap[sq, t].rearrange("(c p) -> p c", p=128), ua_v[:, :, 0, L + t], r=UA_ALL, nonc=True, final=True)
                    for t in range(3):
                        dma(o_cqp[sq, t].rearrange("(c p) -> p c", p=128), pre_v[:, :, 0, L + t], r=PRE_ALL, nonc=True, final=True)
                else:
                    cp("pool", ua_v[:, :, :, 0:2], ua_v[:, :, :, L:L + 2], r=UA_ALL, w=UA_ALL)
                    cp("pool", pre_v[:, :, :, 0:3], pre_v[:, :, :, L:L + 3], r=PRE_ALL, w=PRE_ALL)
            else:
                for s in range(4):
                    for t in range(2):
                        dma(o_cas[s, t].rearrange("(c p) -> p c", p=128), ua_v[:, :, s, L + t], r=UA_ALL, nonc=True, final=True)
                    for t in range(3):
                        dma(o_cqs[s, t].rearrange("(c p) -> p c", p=128), pre_v[:, :, s, L + t], r=PRE_ALL, nonc=True, final=True)

        def tile_units(bi, j, nseg, sbase, nlev):
            pb = bi % 2
            tp_ = j % 2
            qk, sz, mx = qkvT[pb], szg[pb], mixT[pb]
            tsl = slice(j * 128, (j + 1) * 128)
            HS = range(4)
            if nseg == 1:
                MI, MS, SL = cs(C_PMI), cs(C_PMS), cs(C_PSL)
            else:
                MI, MS, SL = cs(C_SMI), cs(C_SMS), cs(C_SSL)
            abr = "ab%d_%d" % (pb, j)
            T = TINY[tp_]

            def n(x):
                return "%s_%d" % (x, tp_)
            g_t, beta, nbeta, egb, negb, ekd, egl = (T["g_t"], T["beta"], T["nbeta"], T["egb"], T["negb"],
                                                    T["ekd"], T["egl"])

            def prepA():
                tt("dve", T["abx"][:], ab_sb[pb][:, j, 0:4], dtb_bc[:], ALU.add, r=[abr, "dtb"], w=[n("abx")])
                act(T["e2"][:], ab_sb[pb][:, j, 4:8], AF.Exp, scale=-1.0, r=[abr], w=[n("e2")])
                act(T["e1"][:], T["abx"][:], AF.Exp, r=[n("abx")], w=[n("e1")])
                act(T["sp_t"][:], T["e1"][:], AF.Ln, bias=1.0, r=[n("e1")], w=[n("sp_t")])
                tt("dve", g_t[:], T["sp_t"][:], negA_bc[:], ALU.mult, r=[n("sp_t"), "negA"], w=[n("g_t")])
                ts("dve", nbeta[:], T["e2"][:], -1.0, ALU.mult, -1.0, ALU.add, r=[n("e2")], w=[n("nbeta")])
                S.op("dve", lambda e: e.reciprocal(out=nbeta[:], in_=nbeta[:]), [n("nbeta")], [n("nbeta")])
                ts("dve", beta[:], nbeta[:], -1.0, ALU.mult, r=[n("nbeta")], w=[n("beta")])
                if nseg == 1:
                    gsv = g_t[:]
                else:
                    for s in range(nseg):
                        ts("dve", T["gseg"][:, s * 4:(s + 1) * 4], g_t[:], cst[:, C_SEG + s:C_SEG + s + 1], ALU.mult,
                           r=[n("g_t"), "cst"], w=[n("gseg")])
                    gsv = T["gseg"][:, 0:4 * nseg]
                po = 32 * tp_
                mms([(pTiny[:, po + 0:po + 4], MI, g_t[:], True, True),
                     (pTiny[:, po + 4:po + 8], SL, g_t[:], True, True),
                     (pTiny[:, po + 8:po + 8 + 4 * nseg], cs(C_ONE), gsv, True, True)],
                    r=["cst", n("g_t"), n("gseg")], w=[R_TINY])
                act(egb[:], pTiny[:, po + 0:po + 4], AF.Exp, r=[R_TINY], w=[n("egb")])
                act(ekd[:], pTiny[:, po + 4:po + 8], AF.Exp, r=[R_TINY], w=[n("ekd")])
                act(egl[:, 0:4 * nseg], pTiny[:, po + 8:po + 8 + 4 * nseg], AF.Exp, r=[R_TINY], w=[n("egl")])
                ts("dve", negb[:], egb[:], -1.0, ALU.mult, r=[n("egb")], w=[n("negb")])

            def prepB(hp):
                def f():
                    hh = (2 * hp, 2 * hp + 1)
                    for h in hh:
                        ts("pool", gUs[:, h, :], MI, g_t[:, h:h + 1], ALU.mult, r=["cst", n("g_t")], w=["gUs%d" % h])
                    mms([(pD[:, 0:256], SL, gUs[:, 2 * hp:2 * hp + 2, :].rearrange("p h i -> p (h i)"), True, True)],
                        r=["cst"] + ["gUs%d" % h for h in hh], w=[BK_D])
                    act(EG[:, 2 * hp:2 * hp + 2, :].rearrange("p h i -> p (h i)"), pD[:, 0:256], AF.Exp, r=[BK_D],
                        w=["EG%d" % h for h in hh])
                    for h in hh:
                        tt("pool", gUs[:, h, :], EG[:, h, :], MS, ALU.mult, r=["EG%d" % h, "cst"], w=["gUs%d" % h])
                        tt("pool", EG[:, h, :], EG[:, h, :], MI, ALU.mult, r=["EG%d" % h, "cst"], w=["EG%d" % h])
                return f
            GTs, GTi = gUs, EG

            def slots(h):
                pb_ = pHb[h]
                return dict(KV=pb_[:, 0:128].bitcast(BF16), TR=pb_[:, 448:512].bitcast(BF16), KQ=pb_[:, 0:256],
                            NM=pb_[:, 0:384], B=pb_[:, 0:128], C=pb_[:, 128:256], A=pb_[:, 256:384])

            PS_ = [slots(h) for h in HS]
            qTs = [qk[:, h, tsl] for h in HS]
            kTs = [qk[:, 4 + h, tsl] for h in HS]
            vTs = [qk[:, 8 + h, tsl] for h in HS]
            qr = ["qkvT%d_%d" % (pb, h) for h in HS]
            kr = ["qkvT%d_%d" % (pb, 4 + h) for h in HS]
            vr = ["qkvT%d_%d" % (pb, 8 + h) for h in HS]
            main = []

            def P1():
                for h in HS:
                    trs([(PS_[h]["KV"][:, 0:128], kTs[h]), (PS_[h]["KV"][:, 128:256], vTs[h])],
                        r=[kr[h], vr[h]], w=[BK_H[h]])
                    act(junkh[h][:], PS_[h]["KV"][:, 0:128], AF.Square, accum=ssk[:, h:h + 1], r=[BK_H[h]],
                        w=["junkh%d" % h, "ssk%d" % h])
                    tt("pool", H[h]["qsq"][:], qTs[h], qTs[h], ALU.mult, r=[qr[h]], w=["qsq%d" % h])
                act(lk[:], ssk[:], AF.Ln, bias=1e-6, r=["ssk%d" % h for h in HS], w=["lk"])
                act(rk[:], lk[:], AF.Exp, scale=-0.5, r=["lk"], w=["rk"])
            main.append(P1)

            def P2():
                for h in HS:
                    ts("dve", H[h]["khat"][:], PS_[h]["KV"][:, 0:128], rk[:, h:h + 1], ALU.mult, r=[BK_H[h], "rk"],
                       w=["khat%d" % h])
                    cp("act", H[h]["vtm"][:], PS_[h]["KV"][:, 128:256], r=[BK_H[h]], w=["vtm%d" % h])
                po = 16 + 32 * tp_
                mms([(pTiny[:, po + 2 * h:po + 2 * h + 2], H[h]["qsq"][:], ones_bf[:], True, True) for h in HS],
                    r=["qsq%d" % h for h in HS] + ["ones_bf"], w=[BK_D])
                ts("dve", fa[:], pTiny[:, po:po + 8:2], 1.28e-4, ALU.mult, 1.28e-10, ALU.add, r=[BK_D], w=[n("fa")])
            main.append(P2)

            def P3():
                for h in HS:
                    trs([(PS_[h]["TR"][:, 0:128], H[h]["khat"][:])], r=["khat%d" % h], w=[BK_H[h]])
                    cp("act" if h % 2 == 0 else "dve", H[h]["khatT"][:], PS_[h]["TR"][:, 0:128], r=[BK_H[h]],
                       w=["khatT%d" % h])
                    ts("pool", H[h]["kdec"][:], H[h]["khat"][:], ekd[:, h:h + 1], ALU.mult,
                       r=["khat%d" % h, n("ekd")], w=["kdec%d" % h])
            main.append(P3)

            def P4():
                for h in HS:
                    mms([(PS_[h]["KQ"][:, 0:128], H[h]["khatT"][:], H[h]["khatT"][:], True, True),
                         (PS_[h]["KQ"][:, 128:256], H[h]["khatT"][:], qTs[h], True, True)],
                        r=["khatT%d" % h, qr[h]], w=[BK_H[h]])
                    nm0 = H[h]["nm"][0]
                    stt(nm0[:, 128:256], PS_[h]["KQ"][:, 0:128], nbeta[:, h:h + 1], GTs[:, h, :], ALU.mult, ALU.mult,
                        r=[BK_H[h], n("nbeta"), "gUs%d" % h], w=["nm%d_0q" % h])
                    tt("dve", H[h]["attnT"][:], PS_[h]["KQ"][:, 128:256], GTi[:, h, :], ALU.mult,
                       r=[BK_H[h], "EG%d" % h], w=["attnT%d" % h])
            main.append(P4)

            def P5():
                for h in HS:
                    nm0 = H[h]["nm"][0]
                    trs([(PS_[h]["TR"][:, 0:128], nm0[:, 128:256])], r=["nm%d_0q" % h], w=[BK_H[h]])
                    cp("act" if h % 2 == 1 else "dve", nm0[:, 0:128], PS_[h]["TR"][:, 0:128], r=[BK_H[h]],
                       w=["nm%d_0p" % h])
                    tt("pool", nm0[:, 256:384], nm0[:, 128:256], ident_bf[:], ALU.add,
                       r=["nm%d_0q" % h, "ident_bf"], w=["nm%d_0t" % h])
            main.append(P5)

            def NL(lv):
                def f():
                    cur = (lv - 1) % 2
                    fin = lv == nlev + 1
                    for h in HS:
                        a_, b_ = H[h]["nm"][cur], H[h]["nm"][1 - cur]
                        Pp, Qp, Tp = a_[:, 0:128], a_[:, 128:256], a_[:, 256:384]
                        NM = PS_[h]["NM"]
                        lst = []
                        if not fin:
                            lst.append((NM[:, 0:128], Qp, Pp, True, True))
                            lst.append((NM[:, 128:256], Pp, Qp, True, True))
                        lst.append((NM[:, 256:384], ident_bf[:], Tp, True, lv == 1))
                        if lv > 1:
                            lst.append((NM[:, 256:384], Pp, Tp, False, True))
                        rr = ["nm%d_%d%s" % (h, cur, x) for x in "pqt"] + ["ident_bf"]
                        mms(lst, r=rr, w=[BK_H[h]])
                        eng = "act" if h % 2 == 0 else "dve"
                        if fin:
                            cp(eng, b_[:, 256:384], NM[:, 256:384], r=[BK_H[h]], w=["nm%d_%dt" % (h, 1 - cur)])
                        else:
                            cp(eng, b_[:, :], NM[:, :], r=[BK_H[h]], w=["nm%d_%d%s" % (h, 1 - cur, x) for x in "pqt"])
                return f
            for lv in range(1, nlev + 2):
                main.append(NL(lv))
            fi = (nlev + 1) % 2
            TTr = ["nm%d_%dt" % (h, fi) for h in HS]
            TTs = [H[h]["nm"][fi][:, 256:384] for h in HS]
            VN = [H[h]["khat"] for h in HS]
            vnr = ["khat%d" % h for h in HS]

            def S1():
                if nseg > 1:
                    for h in HS:
                        for s in range(nseg):
                            cp("pool", khm[:, h, s, 32 * s:32 * s + 32], H[h]["khatT"][:, 32 * s:32 * s + 32],
                               r=["khatT%d" % h], w=["khm%d" % h])
                            cp("pool", qm[:, h, s, 32 * s:32 * s + 32], qTs[h][:, 32 * s:32 * s + 32], r=[qr[h]],
                               w=["qm%d" % h])
                for h in HS:
                    if nseg == 1:
                        mms([(PS_[h]["B"][:, :], H[h]["khatT"][:], Sbf[:, sbase + h, :], True, True),
                             (PS_[h]["C"][:, :], qTs[h], Sbf[:, sbase + h, :], True, True)],
                            r=["khatT%d" % h, qr[h], "Sbf%d" % (sbase + h)], w=[BK_H[h]])
                    else:
                        mms([(PS_[h]["B"][:, :], khm[:, h, s, :], Sbf[:, s * 4 + h, :], s == 0, s == nseg - 1)
                             for s in range(nseg)] +
                            [(PS_[h]["C"][:, :], qm[:, h, s, :], Sbf[:, s * 4 + h, :], s == 0, s == nseg - 1)
                             for s in range(nseg)],
                            r=["khm%d" % h, "qm%d" % h] + ["Sbf%d" % (s * 4 + h) for s in range(nseg)], w=[BK_H[h]])
                    stt(H[h]["r"][:], PS_[h]["B"][:, :], negb[:, h:h + 1], H[h]["vtm"][:], ALU.mult, ALU.add,
                        r=[BK_H[h], n("negb"), "vtm%d" % h], w=["r%d" % h])
                    act(H[h]["o"][:], PS_[h]["C"][:, :], AF.Copy, scale=egb[:, h:h + 1], r=[BK_H[h], n("egb")],
                        w=["o%d" % h])
            main.append(S1)

            def S2():
                for h in HS:
                    mms([(PS_[h]["A"][:, :], TTs[h], H[h]["r"][:], True, True)], r=[TTr[h], "r%d" % h], w=[BK_H[h]])
                    if h % 2 == 0:
                        act(VN[h][:], PS_[h]["A"][:, :], AF.Copy, scale=beta[:, h:h + 1], r=[BK_H[h], n("beta")],
                            w=[vnr[h]])
                    else:
                        ts("dve", VN[h][:], PS_[h]["A"][:, :], beta[:, h:h + 1], ALU.mult, r=[BK_H[h], n("beta")],
                           w=[vnr[h]])
            main.append(S2)

            def S3():
                for h in HS:
                    si0 = h if nseg > 1 else (sbase + h)
                    if nseg == 1:
                        mms([(PS_[h]["B"][:, :], H[h]["attnT"][:], VN[h][:], True, True),
                             (PS_[h]["C"][:, :], H[h]["kdec"][:], VN[h][:], True, True)],
                            r=["attnT%d" % h, "kdec%d" % h, vnr[h]], w=[BK_H[h]])
                    else:
                        mms([(PS_[h]["B"][:, :], H[h]["attnT"][:], VN[h][:], True, True)],
                            r=["attnT%d" % h, vnr[h]], w=[BK_H[h]])
                    tt("dve", H[h]["o"][:], PS_[h]["B"][:, :], H[h]["o"][:], ALU.add, r=[BK_H[h], "o%d" % h],
                       w=["o%d" % h])
                    if nseg == 1:
                        stt(S32[:, si0, :], S32[:, si0, :], egl[:, h:h + 1], PS_[h]["C"][:, :], ALU.mult, ALU.add,
                            r=["S32_%d" % si0, n("egl"), BK_H[h]], w=["S32_%d" % si0])
                        cp("pool", Sbf[:, si0, :], S32[:, si0, :], r=["S32_%d" % si0], w=["Sbf%d" % si0])
                    act(junkh[h][:], H[h]["o"][:], AF.Square, accum=sso[:, h:h + 1], r=["o%d" % h],
                        w=["junkh%d" % h, "sso%d" % h])
                if nseg > 1:
                    for s in range(nseg):
                        for h in HS:
                            si_ = s * 4 + h
                            ts("pool", H[h]["attnT"][:], H[h]["kdec"][:], cst[:, C_SEG + s:C_SEG + s + 1], ALU.mult,
                               r=["kdec%d" % h, "cst"], w=["attnT%d" % h])
                            mms([(PS_[h]["C"][:, :], H[h]["attnT"][:], VN[h][:], True, True)],
                                r=["attnT%d" % h, vnr[h]], w=[BK_H[h]])
                            stt(S32[:, si_, :], S32[:, si_, :], egl[:, si_:si_ + 1], PS_[h]["C"][:, :], ALU.mult,
                                ALU.add, r=["S32_%d" % si_, n("egl"), BK_H[h]], w=["S32_%d" % si_])
                            cp("pool", Sbf[:, si_, :], S32[:, si_, :], r=["S32_%d" % si_], w=["Sbf%d" % si_])
            main.append(S3)

            def finA():
                stt(fb[:], sso[:], 1.0 / 128, fa[:], ALU.mult, ALU.add, r=["sso%d" % h for h in HS] + [n("fa")],
                    w=["fb"])
                act(fl[:], fb[:], AF.Ln, r=["fb"], w=["fl"])
                act(ff[:], fl[:], AF.Exp, scale=-0.5, r=["fl"], w=["ff"])
                for h in HS:
                    ts("pool", H[h]["r"][:], H[h]["o"][:], ff[:, h:h + 1], ALU.mult, r=["o%d" % h, "ff"],
                       w=["r%d" % h])

            def finB():
                for h in HS:
                    trs([(PS_[h]["TR"][:, 0:128], H[h]["r"][:])], r=["r%d" % h], w=[BK_H[h]])
                    stt(mx[:, 4 + h, tsl], PS_[h]["TR"][:, 0:128], gnw_t[:, 0:1], sz[:, h, tsl], ALU.mult, ALU.mult,
                        r=[BK_H[h], "gnw", "szg%d_%d" % (pb, h)], w=["mixT%d_%d" % (pb, 4 + h)])
            return dict(prepA=prepA, prepB=[prepB(0), prepB(1)], main=main, finA=finA, finB=finB)

        pOb = pT[:].rearrange("p a b -> p (a b)").bitcast(F32)

        def d_units(bi, j, dst_ap):
            buf = bi % 2
            mx = mixT[buf]
            tsl = slice(j * 128, (j + 1) * 128)
            yt = ytmp[j % 2]
            yr = "ytmp%d" % (j % 2)

            def half(hf):
                def f():
                    mms([(pOb[:, :], mx[:, k, tsl], w_out_bf[:, k, hf * 512:(hf + 1) * 512], k == 0, k == 7)
                         for k in range(8)], r=MIX_ALL(buf) + ["w_out_bf"], w=[BK_T])
                    act(junk[:, 0:512], pOb[:, :], AF.Square, accum=ss2[:, hf:hf + 1], r=[BK_T],
                        w=["junk", "ss2_%d" % hf])
                    cp("dve", yt[:, hf * 512:(hf + 1) * 512], pOb[:, :], r=[BK_T], w=[yr])
                return f

            def tail():
                tt("dve", ss2s[:], ss2[:, 0:1], ss2[:, 1:2], ALU.add, r=["ss2_0", "ss2_1"], w=["ss2s"])
                act(lt2[:], ss2s[:], AF.Ln, scale=1.0 / D, bias=1e-6, r=["ss2s"], w=["lt2"])
                act(rstd2[:], lt2[:], AF.Exp, scale=-0.5, r=["lt2"], w=["rstd2"])
                stt(yt[:], yt[:], rstd2[:, 0:1], lnpost_bc[:], ALU.mult, ALU.mult, r=[yr, "rstd2", "lnpost"], w=[yr])
                tt("pool", yt[:], yt[:], xt[buf][:, j, :], ALU.add, r=[yr, "xt%d" % buf], w=[yr])
                dma(dst_ap[j * 128:(j + 1) * 128, :], yt[:], r=[yr], final=True)
            return [half(0), half(1), tail]

        blocks = []
        for sq in range(n_pseq):
            for b in range(nblk):
                blocks.append(dict(kind="p", sq=sq, b=b, ntile=2, ntok=NT, nseg=1, nlev=6,
                                   src=xp[sq, b * NT:(b + 1) * NT, :], dst=yp[sq, b * NT:(b + 1) * NT, :],
                                   first=(b == 0), last=(b == nblk - 1)))
        if n_sseq:
            blocks.append(dict(kind="s", ntile=1, ntok=128, nseg=4, nlev=4, src=xs, dst=ys, first=True, last=True))

        def x_stream(bi):
            blk = blocks[bi]
            if blk["kind"] == "p":
                if blk["first"]:
                    memset("pool", ua[:], 0.0, w=UA_ALL)
                    memset("pool", pre[:], 0.0, w=PRE_ALL)
            else:
                ua_s = ua[:, 0:4 * 4 * (2 + 32)].rearrange("p (c s l) -> p c s l", c=4, s=4)
                pre_s = pre[:, 0:12 * 4 * (3 + 32)].rearrange("p (c s l) -> p c s l", c=12, s=4)
                for s in range(4):
                    for t in range(2):
                        dma(ua_s[:, :, s, t], cca[s, t].rearrange("(c p) -> p c", p=128), w=UA_ALL, nonc=True)
                    for t in range(3):
                        dma(pre_s[:, :, s, t], ccq[s, t].rearrange("(c p) -> p c", p=128), w=PRE_ALL, nonc=True)
            yield from gen_a(bi, blk["ntile"])
            yield from gen_b(bi, blk["ntok"], blk["nseg"])
            conv_tail(blk, blk["nseg"], blk["ntok"] // blk["nseg"])
            yield

        def y_stream(bi):
            blk = blocks[bi]
            if blk["kind"] == "p":
                if blk["first"]:
                    for h in range(4):
                        memset("pool", S32[:, h, :], 0.0, w=["S32_%d" % h])
                        memset("pool", Sbf[:, h, :], 0.0, w=["Sbf%d" % h])
            else:
                for s in range(4):
                    dma(S32[:, s * 4:(s + 1) * 4, :], sg[s].rearrange("h k v -> k h v"),
                        w=["S32_%d" % (s * 4 + h) for h in range(4)])
                for i in range(16):
                    cp("pool", Sbf[:, i, :], S32[:, i, :], r=["S32_%d" % i], w=["Sbf%d" % i])
            nt_ = blk["ntile"]
            TU = [tile_units(bi, j, blk["nseg"], 0, blk["nlev"]) for j in range(nt_)]
            DU = [d_units(bi, j, blk["dst"]) for j in range(nt_)]
            seq = []
            for j in range(nt_):
                seq.append(TU[j]["prepA"])
            seq += TU[0]["prepB"]
            if nt_ == 1:
                seq += TU[0]["main"] + [TU[0]["finA"], TU[0]["finB"]] + DU[0]
            else:
                m0, m1 = TU[0]["main"], TU[1]["main"]
                seq += m0[0:6] + [TU[1]["prepB"][0]] + m0[6:7] + [TU[1]["prepB"][1]] + m0[7:]
                seq += [TU[0]["finA"]] + m1[0:2] + [TU[0]["finB"]] + m1[2:6]
                rest = m1[6:]
                seq += rest[0:1] + DU[0][0:1] + rest[1:2] + DU[0][1:2] + rest[2:3] + DU[0][2:3] + rest[3:]
                seq += [TU[1]["finA"], TU[1]["finB"]] + DU[1]
            d_all = [u for du in DU for u in du]
            neu = [u for j in range(nt_) for u in TU[j]["main"][5:5 + blk["nlev"] + 1]]
            wts = [(W_NEU if any(u is x_ for x_ in neu) else 1.0) for u in seq]
            wtot = sum(wts)
            acc = 0.0
            for k, u in enumerate(seq):
                yield (bi, any(u is d_ for d_ in d_all), acc / wtot)
                acc += wts[k]
                u()
                if blk["last"] and u is TU[nt_ - 1]["main"][-1]:
                    if blk["kind"] == "p":
                        dma(o_gp[blk["sq"]].rearrange("h k v -> k h v"), S32[:, 0:4, :],
                            r=["S32_%d" % h for h in range(4)], final=True)
                    else:
                        for s in range(4):
                            dma(o_gs[s].rearrange("h k v -> k h v"), S32[:, s * 4:(s + 1) * 4, :],
                                r=["S32_%d" % (s * 4 + h) for h in range(4)], final=True)

        NXU, LEAD, PREFETCH_AT, X_PARTIAL, W_NEU = 43.0, 0.75, 34, 26, 0.65
        nb = len(blocks)
        xunits = [0] * nb
        xfull = [False] * nb

        x_need = [-1]

        def x_all():
            load_x(0, blocks[0]["src"], blocks[0]["ntile"])
            for bi_ in range(nb):
                k = 0
                pref = False
                for _ in x_stream(bi_):
                    k += 1
                    xunits[bi_] = k
                    if k == PREFETCH_AT and bi_ + 1 < nb:
                        load_x(bi_ + 1, blocks[bi_ + 1]["src"], blocks[bi_ + 1]["ntile"])
                        pref = True
                    x_need[0] = (bi_ - 1) if (k == PREFETCH_AT - 1) else -1
                    yield bi_ + min(k / NXU, 0.999)
                xfull[bi_] = True
                x_need[0] = bi_ - 1
                yield bi_ + 1.0
                if not pref and bi_ + 1 < nb:
                    load_x(bi_ + 1, blocks[bi_ + 1]["src"], blocks[bi_ + 1]["ntile"])

        def y_all():
            for bi_ in range(nb):
                for (b_, full_, frac_) in y_stream(bi_):
                    yield (bi_ + min(frac_, 0.999), b_, full_)

        xg, yg = x_all(), y_all()
        xpos = 0.0
        xdone = False
        _end = object()
        ynext = next(yg, _end)

        def y_complete(b_):
            return b_ < 0 or ynext is _end or ynext[1] > b_

        while ynext is not _end or not xdone:
            if ynext is not _end:
                ypos, yb, yfull = ynext
                sat = xfull[yb] or ((not yfull) and xunits[yb] >= X_PARTIAL)
            else:
                sat = False
            x_ok = (not xdone) and y_complete(x_need[0])
            if sat and (not x_ok or ypos + LEAD <= xpos):
                ynext = next(yg, _end)
            elif x_ok:
                v = next(xg, _end)
                if v is _end:
                    xdone = True
                else:
                    xpos = v
            else:
                raise AssertionError("pipeline driver deadlock")
        S.emit(final_wait_ops=final_ops)
    return nc


_CACHE = {}
SB_BYTES = [0]


def _get_program(key):
    if key not in _CACHE:
        _CACHE[key] = build_program(*key)
    return _CACHE[key]


def make_in_maps(x_prompt, x_sample, cache_conv_a, cache_conv_qkv, state_gdn, ln_pre, w_in, conv_a_w,
                 conv_qkv_w, a_log, dt_bias, gdn_norm_w, w_out, ln_post, n_cores, n_pseq, n_sseq):
    f = lambda a: np.ascontiguousarray(np.asarray(a, dtype=np.float32))
    cst = make_consts()
    maps = []
    for c in range(n_cores):
        maps.append({
            "xp": f(x_prompt[c * n_pseq:(c + 1) * n_pseq]),
            "xs": f(x_sample[c * n_sseq:(c + 1) * n_sseq]).reshape(-1, D),
            "cca": f(cache_conv_a[0, c * n_sseq:(c + 1) * n_sseq]),
            "ccq": f(cache_conv_qkv[0, c * n_sseq:(c + 1) * n_sseq]),
            "sg": f(state_gdn[0, c * n_sseq:(c + 1) * n_sseq]),
            "ln_pre": f(ln_pre[0]), "w_in": f(w_in[0]), "caw": f(conv_a_w[0]), "cqw": f(conv_qkv_w[0]),
            "a_log": f(a_log[0]), "dt_bias": f(dt_bias[0]), "gnw": f(gdn_norm_w[0]),
            "w_out": f(w_out[0]), "ln_post": f(ln_post[0]), "cst": cst,
        })
    return maps


def kernel(x_prompt, x_sample, cache_conv_a, cache_conv_qkv, state_gdn, ln_pre, w_in, conv_a_w,
           conv_qkv_w, a_log, dt_bias, gdn_norm_w, w_out, ln_post):
    x_prompt = np.asarray(x_prompt)
    x_sample = np.asarray(x_sample)
    B, T, _ = x_prompt.shape
    Bs, Ts, _ = x_sample.shape
    n_pseq = B // NCORES
    n_sseq = Bs // NCORES
    nc = _get_program((n_pseq, T, n_sseq, Ts))
    maps = make_in_maps(x_prompt, x_sample, cache_conv_a, cache_conv_qkv, state_gdn, ln_pre, w_in, conv_a_w,
                        conv_qkv_w, a_log, dt_bias, gdn_norm_w, w_out, ln_post, NCORES, n_pseq, n_sseq)
    res = run_bass_kernel_spmd(nc, maps, core_ids=list(range(NCORES))).results
    cat = lambda k: np.concatenate([np.asarray(r[k]) for r in res], axis=0)
    y_p = cat("yp")
    y_s = cat("ys").reshape(Bs, Ts, D)
    return (y_p, y_s, cat("o_cap")[None], cat("o_cqp")[None], cat("o_gp")[None],
            cat("o_cas")[None], cat("o_cqs")[None], cat("o_gs")[None])
```
